# Optimizing a Trainium2 kernel written in Bass

```python
import jax, jax.numpy as jnp
from jax import lax
import numpy as np

D_MODEL = 1024
BATCH = 4
SEQ = 8192
DEPTH = 4

D_RNN = D_MODEL
RG_HEADS = 4
RG_BLOCK = D_RNN // RG_HEADS
RG_C = 8.0
CONV4_WIDTH = 4
D_CONV = D_MODEL
CONV3_WIDTH = 3
D_FF = ((8 * D_MODEL + 3 * 256 - 1) // (3 * 256)) * 256
PLE_DIM = 256
EPS = 1e-6
IN_SPLITS = (D_RNN, D_RNN, D_CONV, D_CONV, D_CONV, D_MODEL, D_MODEL)
W_IN = sum(IN_SPLITS)
IN_OFFSETS = tuple(int(v) for v in np.cumsum(IN_SPLITS)[:-1])

kernel_name = "hybrid_rglru_shortconv_block"


def rmsnorm(x, g):
    xf = x.astype(jnp.float32)
    y = xf * lax.rsqrt(jnp.mean(xf * xf, axis=-1, keepdims=True) + EPS)
    return (y * g.astype(jnp.float32)).astype(x.dtype)


def causal_depthwise_conv(x, w):
    k_w = w.shape[0]
    t = x.shape[1]
    xp = jnp.pad(x, ((0, 0), (k_w - 1, 0), (0, 0)))
    y = xp[:, 0:t] * w[0]
    for k in range(1, k_w):
        y = y + xp[:, k:k + t] * w[k]
    return y


def block_diag_linear(x, w, b):
    bsz, t, _ = x.shape
    xh = x.reshape(bsz, t, RG_HEADS, RG_BLOCK)
    y = jnp.einsum("bthi,hij->bthj", xh, w) + b
    return y.reshape(bsz, t, RG_HEADS * RG_BLOCK)


def rg_lru(x, w_r, b_r, w_i, b_i, lam):
    r = jax.nn.sigmoid(block_diag_linear(x, w_r, b_r).astype(jnp.float32))
    i = jax.nn.sigmoid(block_diag_linear(x, w_i, b_i).astype(jnp.float32))
    log_a = -RG_C * r * jax.nn.softplus(-lam.astype(jnp.float32))
    a = jnp.exp(log_a)
    mult = jnp.sqrt(-jnp.expm1(2.0 * log_a))
    u = mult * (i * x.astype(jnp.float32))

    def step(h, au):
        a_t, u_t = au
        h = a_t * h + u_t
        return h, h

    h0 = jnp.zeros((x.shape[0], x.shape[2]), jnp.float32)
    _, hs = lax.scan(step, h0, (jnp.swapaxes(a, 0, 1), jnp.swapaxes(u, 0, 1)))
    return jnp.swapaxes(hs, 0, 1).astype(x.dtype)


def hybrid_layer(x, p_i, g_mix, w_in, conv4_w, conv4_b, w_rg_r, b_rg_r, w_rg_i, b_rg_i,
                 lru_lambda, conv3_w, w_out, g_ffn, w_gate_up, w_down, g_ple, w_ple_gate, w_ple):
    h = rmsnorm(x, g_mix)
    z = h @ w_in
    rnn_x, rnn_y, conv_b, conv_c, conv_x, gate_rnn, gate_conv = jnp.split(z, IN_OFFSETS, axis=-1)
    rnn_x = causal_depthwise_conv(rnn_x, conv4_w) + conv4_b
    y_rnn = jax.nn.gelu(rnn_y) * rg_lru(rnn_x, w_rg_r, b_rg_r, w_rg_i, b_rg_i, lru_lambda)
    y_conv = conv_b * causal_depthwise_conv(conv_c * conv_x, conv3_w)
    merged = jax.nn.sigmoid(gate_rnn) * y_rnn + jax.nn.sigmoid(gate_conv) * y_conv
    x = x + merged @ w_out
    h = rmsnorm(x, g_ffn)
    g, u = jnp.split(h @ w_gate_up, 2, axis=-1)
    x = x + (jax.nn.silu(g) * u) @ w_down
    gate = jax.nn.sigmoid(rmsnorm(x, g_ple) @ w_ple_gate)
    x = x + gate * (p_i @ w_ple)
    return x


def setup_inputs(seed: int = 0) -> dict:
    key = jax.random.key(seed)
    ks = jax.random.split(key, 20)

    def nrm(k, shape, fan_in):
        return jax.random.normal(k, shape, jnp.float32) * (fan_in ** -0.5)

    def gain(k, shape):
        return 1.0 + 0.05 * jax.random.normal(k, shape, jnp.float32)

    def bias(k, shape):
        return 0.02 * jax.random.normal(k, shape, jnp.float32)

    x = jax.random.normal(ks[0], (BATCH, SEQ, D_MODEL), jnp.float32)
    p = jax.random.normal(ks[1], (DEPTH, BATCH, SEQ, PLE_DIM), jnp.float32)
    g_mix = gain(ks[2], (DEPTH, D_MODEL))
    w_in = nrm(ks[3], (DEPTH, D_MODEL, W_IN), D_MODEL)
    conv4_w = nrm(ks[4], (DEPTH, CONV4_WIDTH, D_RNN), CONV4_WIDTH)
    conv4_b = bias(ks[5], (DEPTH, D_RNN))
    w_rg_r = nrm(ks[6], (DEPTH, RG_HEADS, RG_BLOCK, RG_BLOCK), RG_BLOCK)
    b_rg_r = bias(ks[7], (DEPTH, RG_HEADS, RG_BLOCK))
    w_rg_i = nrm(ks[8], (DEPTH, RG_HEADS, RG_BLOCK, RG_BLOCK), RG_BLOCK)
    b_rg_i = bias(ks[9], (DEPTH, RG_HEADS, RG_BLOCK))
    a_c = jax.random.uniform(ks[10], (DEPTH, D_RNN), jnp.float32, minval=0.9, maxval=0.999)
    a0 = a_c ** (1.0 / RG_C)
    lru_lambda = jnp.log(a0) - jnp.log1p(-a0)
    conv3_w = nrm(ks[11], (DEPTH, CONV3_WIDTH, D_CONV), CONV3_WIDTH)
    w_out = nrm(ks[12], (DEPTH, D_MODEL, D_MODEL), D_MODEL)
    g_ffn = gain(ks[13], (DEPTH, D_MODEL))
    w_gate_up = nrm(ks[14], (DEPTH, D_MODEL, 2 * D_FF), D_MODEL)
    w_down = nrm(ks[15], (DEPTH, D_FF, D_MODEL), D_FF)
    g_ple = gain(ks[16], (DEPTH, D_MODEL))
    w_ple_gate = nrm(ks[17], (DEPTH, D_MODEL, D_MODEL), D_MODEL)
    w_ple = nrm(ks[18], (DEPTH, PLE_DIM, D_MODEL), PLE_DIM)
    g_final = gain(ks[19], (D_MODEL,))
    return {"x": x, "p": p, "g_mix": g_mix, "w_in": w_in, "conv4_w": conv4_w,
            "conv4_b": conv4_b, "w_rg_r": w_rg_r, "b_rg_r": b_rg_r, "w_rg_i": w_rg_i,
            "b_rg_i": b_rg_i, "lru_lambda": lru_lambda, "conv3_w": conv3_w, "w_out": w_out,
            "g_ffn": g_ffn, "w_gate_up": w_gate_up, "w_down": w_down, "g_ple": g_ple,
            "w_ple_gate": w_ple_gate, "w_ple": w_ple, "g_final": g_final}


def reference(x, p, g_mix, w_in, conv4_w, conv4_b, w_rg_r, b_rg_r, w_rg_i, b_rg_i,
              lru_lambda, conv3_w, w_out, g_ffn, w_gate_up, w_down, g_ple, w_ple_gate,
              w_ple, g_final):
    for i in range(DEPTH):
        x = hybrid_layer(x, p[i], g_mix[i], w_in[i], conv4_w[i], conv4_b[i], w_rg_r[i],
                         b_rg_r[i], w_rg_i[i], b_rg_i[i], lru_lambda[i], conv3_w[i], w_out[i],
                         g_ffn[i], w_gate_up[i], w_down[i], g_ple[i], w_ple_gate[i], w_ple[i])
    return rmsnorm(x, g_final)
```

```python
import math
from contextlib import ExitStack

import numpy as np
import concourse.bass as bass
import concourse.mybir as mybir
from concourse.bass_utils import run_bass_kernel_spmd

F32 = mybir.dt.float32
BF16 = mybir.dt.bfloat16
AF = mybir.ActivationFunctionType
ALU = mybir.AluOpType

D = 1024
NFC = 8
DFF = 2816
NJ = 22
PLE = 256
DEPTH = 4
SEQ = 8192
BATCH = 4
TC = 1024
W = 512
NT = TC // W
EPS = 1e-6
NSLOT = 8
NSCR = 12
import os
PENG = os.environ.get('PENG', 'dve')
DEFER = int(os.environ.get('DEFER', '15'))
KAHEAD = int(os.environ.get('KAHEAD', '2'))

C_GMIX, C_GFFN, C_GPLE, C_C4W, C_C4B, C_BRR, C_BRI, C_LAM, C_C3W = 0, 8, 16, 24, 56, 64, 72, 80, 88
C_HC, C_HBR, C_HBI, C_C3W2, C_TMP = 112, 120, 128, 136, 160
LW = 168
GELU_K0 = math.sqrt(2.0 / math.pi)
GELU_K1 = GELU_K0 * 0.044715


class Prog:
    ENGS = ("pe", "act", "dve", "pool", "sp")

    def __init__(self, nc):
        self.nc = nc
        self.ops = {e: [] for e in self.ENGS}
        self.cnt = {}
        self.seen = {}
        self.last_w = {}
        self.readers = {}
        self.semkeys = []
        for e in self.ENGS[:4]:
            self._sem(e)

    def _sem(self, key):
        if key not in self.cnt:
            self.cnt[key] = 0
            self.semkeys.append(key)

    def _deps(self, reads, writes):
        deps = []
        for r in reads:
            t = self.last_w.get(r)
            if t is not None:
                deps.append(t)
        for w in writes:
            t = self.last_w.get(w)
            if t is not None:
                deps.append(t)
            deps.extend(self.readers.get(w, ()))
        return deps

    def _commit(self, token, reads, writes):
        for r in reads:
            self.readers.setdefault(r, []).append(token)
        for w in writes:
            self.last_w[w] = token
            self.readers[w] = []

    def _waits(self, eng, deps, token):
        waits = {}
        for (k, v) in deps:
            if (k, v) == token:
                continue
            if k == "pe" and eng == "pe":
                continue
            if self.seen.get((eng, k), 0) >= v:
                continue
            if waits.get(k, 0) < v:
                waits[k] = v
        for k, v in waits.items():
            self.seen[(eng, k)] = v
        return list(waits.items())

    def op(self, eng, fn, reads=(), writes=(), inc=True):
        deps = self._deps(reads, writes)
        if inc:
            self.cnt[eng] += 1
            token = (eng, self.cnt[eng])
        else:
            token = (eng, self.cnt[eng] + 1)
        waits = self._waits(eng, deps, token)
        self.ops[eng].append((waits, fn, (eng, 1) if inc else None))
        self._commit(token, reads, writes)
        return token

    def dma(self, queue, chan, fn, reads=(), writes=()):
        key = ("dma", chan)
        self._sem(key)
        deps = self._deps(reads, writes)
        if self.cnt[key] > 0:
            deps.append((key, self.cnt[key]))
        self.cnt[key] += 16
        token = (key, self.cnt[key])
        waits = self._waits(queue, deps, token)
        self.ops[queue].append((waits, fn, (key, 16)))
        self._commit(token, reads, writes)
        return token

    def cc(self, queue, chan, fn, reads=(), writes=()):
        key = ("cc", chan)
        self._sem(key)
        deps = self._deps(reads, writes)
        if self.cnt[key] > 0:
            deps.append((key, self.cnt[key]))
        self.cnt[key] += 1
        token = (key, self.cnt[key])
        waits = self._waits(queue, deps, token)
        self.ops[queue].append((waits, fn, (key, 1)))
        self._commit(token, reads, writes)
        return token

    def wait_all(self, eng, tokens):
        waits = self._waits(eng, tokens, None)
        self.ops[eng].append((waits, None, None))

    def emit(self):
        nc = self.nc
        with ExitStack() as es:
            sems = {}
            for i, k in enumerate(self.semkeys):
                sems[k] = es.enter_context(nc.semaphore("s%d" % i))
            block = es.enter_context(nc.Block())

            def run(engname):
                def body(eng):
                    for waits, fn, inc in self.ops[engname]:
                        for k, v in waits:
                            eng.wait_ge(sems[k], v)
                        if fn is not None:
                            ins = fn(eng)
                            if inc is not None:
                                ins.then_inc(sems[inc[0]], inc[1])
                return body

            block.tensor(run("pe"))
            block.scalar(run("act"))
            block.vector(run("dve"))
            block.gpsimd(run("pool"))
            block.sync(run("sp"))


class DryProg:
    def op(self, *a, **k):
        return None

    def dma(self, *a, **k):
        return None

    def cc(self, *a, **k):
        return None

    def wait_all(self, *a, **k):
        pass


def build_program(nch, nl, pipe, npairs=4):
    layers = list(range(nl))
    ncp = nl * LW + 16
    nc = bass.Bass("TRN2", target_bir_lowering=False)
    dt = nc.dram_tensor
    xT = dt("xT", [nch, NFC, 128, TC], F32, kind="ExternalInput").ap()
    pT = dt("pT", [nl, nch, 2, 128, TC], F32, kind="ExternalInput").ap()
    cp = dt("cp", [128, ncp], F32, kind="ExternalInput").ap()
    w_in = dt("w_in", [nl, D, 7 * D], F32, kind="ExternalInput").ap()
    w_rg_r = dt("w_rg_r", [nl, 4, 256, 256], F32, kind="ExternalInput").ap()
    w_rg_i = dt("w_rg_i", [nl, 4, 256, 256], F32, kind="ExternalInput").ap()
    w_out = dt("w_out", [nl, D, D], F32, kind="ExternalInput").ap()
    w_gu = dt("w_gate_up", [nl, D, 2 * DFF], F32, kind="ExternalInput").ap()
    w_dn = dt("w_down", [nl, DFF, D], F32, kind="ExternalInput").ap()
    w_pg = dt("w_ple_gate", [nl, D, D], F32, kind="ExternalInput").ap()
    w_pl = dt("w_ple", [nl, PLE, D], F32, kind="ExternalInput").ap()
    oT = dt("oT", [nch, NFC, 128, TC], F32, kind="ExternalOutput").ap()
    if pipe:
        exin = [nc.dram_tensor("exin%d" % q, [256, TC], F32) for q in range(4)]
        exout = [nc.dram_tensor("exout%d" % q, [512, TC], F32) for q in range(4)]

    es = ExitStack()
    with es:
        def sb(name, shape, dtype):
            return es.enter_context(nc.sbuf_tensor(name, shape, dtype))

        xres = sb("xres", [128, NFC, TC], F32)
        hT = sb("hT", [128, NFC, TC], BF16)
        sq = sb("sq", [128, NFC, W], BF16)
        actT = sb("actT", [128, NJ, TC], BF16)
        ptl = sb("ptl", [128, 2, 2, TC], BF16)
        ring = sb("ring", [128, NSLOT, 8 * 256], BF16)
        cpt = sb("cpt", [128, ncp], F32)
        cst = sb("cst", [128, 4], F32)
        ones = sb("ones", [128, 128], BF16)
        scr = sb("scr", [128, NSCR, 516], F32)
        llA = sb("llA", [128, 4, W], F32)
        llB = sb("llB", [128, 4, W], F32)
        llC = sb("llC", [128, 4, W], F32)
        llD = sb("llD", [128, 4, W], F32)
        xcb = sb("xcb", [128, 4, W], BF16)
        rstd = sb("rstd", [128, 2, W], F32)
        hst = sb("hst", [128, nl, NFC], F32)
        c4h = sb("c4h", [128, nl, NFC, 3], F32)
        c3h = sb("c3h", [128, nl, NFC, 2], F32)
        psb = [es.enter_context(nc.psum_tensor("ps%d" % i, [128, W], F32)) for i in range(8)]

        def emit_all(P, rec, specs):
            st = {"ps": 0, "scr": 0, "ws": 0, "pbuf": 0, "hcnt": 0, "wn": 0, "wi": 0}

            def next_ps():
                b = st["ps"]
                st["ps"] = (b + 1) % 8
                return b

            def next_scr():
                i = st["scr"]
                st["scr"] = (i + 1) % NSCR
                return i

            EPS_AP = cst[:, 0:1]
            ONE_AP = cst[:, 1:2]
            ZERO_AP = cst[:, 2:3]

            P.op("dve", lambda e: e.memset(cst[:, 0:1], EPS), writes=["cst"])
            P.op("dve", lambda e: e.memset(cst[:, 1:2], 1.0), writes=["cst"])
            P.op("dve", lambda e: e.memset(cst[:, 2:4], 0.0), writes=["cst"])
            P.op("dve", lambda e: e.memset(ones[:], 1.0 / D), writes=["ones"])
            P.op("dve", lambda e: e.memset(hst[:], 0.0), writes=[("hs", l, f) for l in range(nl) for f in range(NFC)])
            P.op("dve", lambda e: e.memset(c4h[:], 0.0), writes=[("c4", l, f) for l in range(nl) for f in range(NFC)])
            P.op("dve", lambda e: e.memset(c3h[:], 0.0), writes=[("c3", l, f) for l in range(nl) for f in range(NFC)])
            P.dma("sp", "cp", lambda e: e.dma_start(out=cpt[:], in_=cp), writes=["cp"])
            for li, L in enumerate(layers):
                o = L * LW
                P.op("act", lambda e, o=o: e.activation(out=cpt[:, o + C_TMP:o + C_TMP + 8], in_=cpt[:, o + C_LAM:o + C_LAM + 8],
                                                          func=AF.Exp, bias=ZERO_AP, scale=-1.0), reads=["cp", "cst"], writes=["cp"])
                P.op("act", lambda e, o=o: e.activation(out=cpt[:, o + C_TMP:o + C_TMP + 8], in_=cpt[:, o + C_TMP:o + C_TMP + 8],
                                                          func=AF.Ln, bias=ONE_AP, scale=1.0), reads=["cp", "cst"], writes=["cp"])
                P.op("dve", lambda e, o=o: e.tensor_scalar(out=cpt[:, o + C_HC:o + C_HC + 8], in0=cpt[:, o + C_TMP:o + C_TMP + 8],
                                                             scalar1=-4.0, scalar2=None, op0=ALU.mult), reads=["cp"], writes=["cp"])
                P.op("dve", lambda e, o=o: e.tensor_scalar(out=cpt[:, o + C_HBR:o + C_HBR + 16], in0=cpt[:, o + C_BRR:o + C_BRR + 16],
                                                             scalar1=0.5, scalar2=None, op0=ALU.mult), reads=["cp"], writes=["cp"])
                P.op("dve", lambda e, o=o: e.tensor_scalar(out=cpt[:, o + C_C3W2:o + C_C3W2 + 24], in0=cpt[:, o + C_C3W:o + C_C3W + 24],
                                                             scalar1=2.0, scalar2=None, op0=ALU.mult), reads=["cp"], writes=["cp"])

            def wslot(dmas):
                if rec is not None:
                    rec.append(dmas)
                    s = st["ws"]
                    st["ws"] = (s + 1) % NSLOT
                    return s
                n = st["wn"]
                st["wn"] += 1
                while st["wi"] <= min(n + KAHEAD, len(specs) - 1):
                    m = st["wi"]
                    st["wi"] += 1
                    sl = m % NSLOT
                    for dst_fn, src in specs[m]:
                        P.dma("pool", "w%d" % sl, lambda e, d=dst_fn(ring[:, sl, :]), src=src: e.dma_start(out=d, in_=src),
                              writes=[("ws", sl)])
                return n % NSLOT

            def kview(ap, nk, ncol):
                return ap[:, 0:nk * ncol].rearrange("p (k n) -> p k n", k=nk)

            def wsrc(mat, r0, nk, c0, ncol):
                return mat[r0:r0 + nk * 128, c0:c0 + ncol].rearrange("(k p) n -> p k n", p=128)

            def mm_group(s, kcs, col0, rhs_fn, rhs_res, bank, first=True, last=True, koff=0):
                n = len(kcs)
                for i, k in enumerate(kcs):
                    P.op("pe", lambda e, k=k, i=i: e.matmul(psb[bank][:], kview(ring[:, s, :], 8, 256)[:, k - koff, col0:col0 + 128],
                                                             rhs_fn(k), start=(first and i == 0), stop=(last and i == n - 1)),
                         reads=[("ws", s)] + rhs_res(k), writes=[("ps", bank)], inc=(i == n - 1))

            def norm_sqs(tt):
                ts = slice(tt * W, (tt + 1) * W)
                for fc in range(NFC):
                    if fc % 2 == 0:
                        P.op("act", lambda e, fc=fc: e.activation(out=sq[:, fc, :], in_=xres[:, fc, ts], func=AF.Square,
                                                                  bias=ZERO_AP, scale=1.0),
                             reads=[("x", fc, tt), "cst"], writes=[("sq", fc)])
                    else:
                        P.op("dve", lambda e, fc=fc: e.tensor_tensor(out=sq[:, fc, :], in0=xres[:, fc, ts], in1=xres[:, fc, ts], op=ALU.mult),
                             reads=[("x", fc, tt)], writes=[("sq", fc)])

            def norm_mm(tt):
                b = next_ps()
                for fc in range(NFC):
                    P.op("pe", lambda e, fc=fc: e.matmul(psb[b][:], ones[:], sq[:, fc, :], start=(fc == 0), stop=(fc == NFC - 1)),
                         reads=["ones", ("sq", fc)], writes=[("ps", b)], inc=(fc == NFC - 1))
                return b

            def norm_fin(gcol, tt, b, out_fn=None, out_res=None):
                ts = slice(tt * W, (tt + 1) * W)
                P.op("act", lambda e: e.activation(out=rstd[:, tt, :], in_=psb[b][:], func=AF.Sqrt, bias=EPS_AP, scale=1.0),
                     reads=[("ps", b), "cst"], writes=[("rstd", tt)])
                P.op("dve", lambda e: e.reciprocal(out=rstd[:, tt, :], in_=rstd[:, tt, :]), reads=[("rstd", tt)], writes=[("rstd", tt)])
                for fc in range(NFC):
                    dst = hT[:, fc, ts] if out_fn is None else out_fn(fc, ts)
                    dres = ("h", fc, tt) if out_fn is None else out_res(fc, tt)
                    P.op("dve", lambda e, fc=fc, dst=dst: e.scalar_tensor_tensor(out=dst, in0=xres[:, fc, ts],
                                                                                 scalar=cpt[:, gcol + fc:gcol + fc + 1], in1=rstd[:, tt, :],
                                                                                 op0=ALU.mult, op1=ALU.mult),
                         reads=[("x", fc, tt), ("rstd", tt), "cp"], writes=[dres])

            def norm(gcol, out_fn=None, out_res=None):
                for tt in range(NT):
                    norm_sqs(tt)
                    b = norm_mm(tt)
                    norm_fin(gcol, tt, b, out_fn, out_res)

            def h_rhs(tt):
                return (lambda k: hT[:, k, tt * W:(tt + 1) * W]), (lambda k: [("h", k, tt)])

            def layer(ch, li, L, first, nxt):
                o = L * LW
                col = lambda c, fc: cpt[:, o + c + fc:o + c + fc + 1]
                if first:
                    norm(o + C_GMIX)
                def do_head(hh, TA, nA, TD, nD):
                    units = [(cc, tt) for tt in range(NT) for cc in range(2)]
                    U = lambda cc, tt: cc * NT + tt
                    TB, nB, TC, nC = llB, "llB", llC, "llC"

                    def zslot(g):
                        return wslot([(lambda d: kview(d, 8, 256), wsrc(w_in[L], 0, 8, g * D + hh * 256, 256))])

                    def zrun(s, evac, tts):
                        pend = None
                        for tt in tts:
                            for cc in range(2):
                                b = next_ps()
                                rf, rr = h_rhs(tt)
                                mm_group(s, list(range(8)), cc * 128, rf, rr, b)
                                d = evac(cc, tt, b)
                                if pend is not None:
                                    pend()
                                pend = d
                        if pend is not None:
                            pend()

                    def zgroup(g, evac):
                        zrun(zslot(g), evac, range(NT))

                    def ev_rnnx(cc, tt, b):
                        fc = hh * 2 + cc
                        u = U(cc, tt)
                        i = next_scr()
                        P.op("act", lambda e: e.activation(out=scr[:, i, 0:3], in_=c4h[:, li, fc, :], func=AF.Copy),
                             reads=[("c4", li, fc)], writes=[("sch", i), ("sc", i)])
                        P.op("act", lambda e: e.activation(out=scr[:, i, 3:3 + W], in_=psb[b][:], func=AF.Copy),
                             reads=[("ps", b)], writes=[("sc", i)])
                        P.op("act", lambda e: e.activation(out=TA[:, u, :], in_=psb[b][:], func=AF.Identity,
                                                           bias=col(C_C4B, fc), scale=col(C_C4W + 24, fc)),
                             reads=[("ps", b), "cp"], writes=[(nA, u)])
                        P.op("act", lambda e: e.activation(out=c4h[:, li, fc, :], in_=scr[:, i, W:W + 3], func=AF.Copy),
                             reads=[("sc", i)], writes=[("c4", li, fc)])
                        def later():
                            for k in (2, 1, 0):
                                P.op("dve", lambda e, k=k: e.scalar_tensor_tensor(out=TA[:, u, :], in0=scr[:, i, k:k + W],
                                                                                  scalar=col(C_C4W + 8 * k, fc), in1=TA[:, u, :],
                                                                                  op0=ALU.mult, op1=ALU.add),
                                     reads=[("sc", i), ("sch", i), (nA, u), "cp"], writes=[(nA, u)])
                            P.op("act", lambda e: e.activation(out=xcb[:, u, :], in_=TA[:, u, :], func=AF.Copy),
                                 reads=[(nA, u)], writes=[("xcb", u)])
                        if DEFER & 1:
                            return later
                        later()
                        return None

                    def ev_grnn(cc, tt, b):
                        u = U(cc, tt)
                        P.op("act", lambda e: e.activation(out=TD[:, u, :], in_=psb[b][:], func=AF.Tanh, bias=ZERO_AP, scale=0.5),
                             reads=[("ps", b), "cst"], writes=[(nD, u)])

                    def ev_rnny(cc, tt, b):
                        u = U(cc, tt)
                        P.op("act", lambda e: e.activation(out=TC[:, u, :], in_=psb[b][:], func=AF.Gelu_apprx_tanh, bias=ZERO_AP, scale=1.0),
                             reads=[("ps", b), "cst"], writes=[(nC, u)])
                        later = lambda: P.op("dve", lambda e: e.scalar_tensor_tensor(out=TC[:, u, :], in0=TD[:, u, :], scalar=1.0, in1=TC[:, u, :],
                                                                                     op0=ALU.add, op1=ALU.mult),
                                             reads=[(nD, u), (nC, u)], writes=[(nC, u)])
                        if DEFER & 2:
                            return later
                        later()
                        return None

                    def ev_convb(cc, tt, b):
                        u = U(cc, tt)
                        P.op("act", lambda e: e.activation(out=TB[:, u, :], in_=psb[b][:], func=AF.Copy),
                             reads=[("ps", b)], writes=[(nB, u)])

                    def ev_convc(cc, tt, b):
                        u = U(cc, tt)
                        P.op("act", lambda e: e.activation(out=TD[:, u, :], in_=psb[b][:], func=AF.Copy),
                             reads=[("ps", b)], writes=[(nD, u)])

                    def ev_convx(cc, tt, b):
                        fc = hh * 2 + cc
                        j = next_scr()
                        u = U(cc, tt)
                        P.op("act", lambda e: e.activation(out=scr[:, j, 0:2], in_=c3h[:, li, fc, :], func=AF.Copy),
                             reads=[("c3", li, fc)], writes=[("sch", j), ("sc", j)])
                        P.op("dve", lambda e: e.tensor_tensor(out=scr[:, j, 2:2 + W], in0=TD[:, u, :], in1=psb[b][:], op=ALU.mult),
                             reads=[(nD, u), ("ps", b)], writes=[("sc", j)])
                        def later():
                            P.op("act", lambda e: e.activation(out=c3h[:, li, fc, :], in_=scr[:, j, W:W + 2], func=AF.Copy),
                                 reads=[("sc", j)], writes=[("c3", li, fc)])
                            P.op("act", lambda e: e.activation(out=TD[:, u, :], in_=scr[:, j, 2:2 + W], func=AF.Identity,
                                                               bias=ZERO_AP, scale=col(C_C3W + 16, fc)),
                                 reads=[("sc", j), "cp", "cst"], writes=[(nD, u)])
                            for k in (1, 0):
                                P.op("dve", lambda e, k=k: e.scalar_tensor_tensor(out=TD[:, u, :], in0=scr[:, j, k:k + W],
                                                                                  scalar=col(C_C3W + 8 * k, fc), in1=TD[:, u, :],
                                                                                  op0=ALU.mult, op1=ALU.add),
                                     reads=[("sc", j), ("sch", j), (nD, u), "cp"], writes=[(nD, u)])
                            P.op(PENG, lambda e: e.tensor_tensor(out=TB[:, u, :], in0=TD[:, u, :], in1=TB[:, u, :], op=ALU.mult),
                                 reads=[(nD, u), (nB, u)], writes=[(nB, u)])
                        if DEFER & 4:
                            return later
                        later()
                        return None

                    def ev_gconv(cc, tt, b):
                        fc = hh * 2 + cc
                        u = U(cc, tt)
                        i = next_scr()
                        P.op("act", lambda e: e.activation(out=scr[:, i, 0:W], in_=psb[b][:], func=AF.Tanh, bias=ZERO_AP, scale=0.5),
                             reads=[("ps", b), "cst"], writes=[("sc", i)])
                        def later():
                            P.op("dve", lambda e: e.scalar_tensor_tensor(out=TB[:, u, :], in0=scr[:, i, 0:W], scalar=1.0, in1=TB[:, u, :],
                                                                         op0=ALU.add, op1=ALU.mult),
                                 reads=[("sc", i), (nB, u)], writes=[(nB, u)])
                            P.op(PENG, lambda e: e.tensor_tensor(out=TA[:, u, :], in0=TA[:, u, :], in1=TC[:, u, :], op=ALU.mult),
                                 reads=[(nA, u), (nC, u)], writes=[(nA, u)])
                            P.op(PENG, lambda e: e.tensor_tensor(out=actT[:, fc, tt * W:(tt + 1) * W], in0=TA[:, u, :], in1=TB[:, u, :], op=ALU.add),
                                 reads=[(nA, u), (nB, u)], writes=[("act", fc, tt)])
                        if DEFER & 8:
                            return later
                        later()
                        return None

                    gsc = {}

                    def gates_a(s, cc, tt):
                        fc = hh * 2 + cc
                        xrf = lambda k: xcb[:, (k % 2) * NT + tt, :]
                        xrr = lambda k: [("xcb", (k % 2) * NT + tt)]
                        br = next_ps()
                        mm_group(s, [0, 1], cc * 128, xrf, xrr, br)
                        bi = next_ps()
                        mm_group(s, [2, 3], cc * 128, xrf, xrr, bi)
                        i1, i2, i3 = next_scr(), next_scr(), next_scr()
                        gsc[(cc, tt)] = (i1, i2, i3)
                        P.op("act", lambda e: e.activation(out=scr[:, i1, 0:W], in_=psb[br][:], func=AF.Tanh, bias=col(C_HBR, fc), scale=0.5),
                             reads=[("ps", br), "cp"], writes=[("sc", i1)])
                        P.op("act", lambda e: e.activation(out=scr[:, i2, 0:W], in_=scr[:, i1, 0:W], func=AF.Exp, bias=col(C_HC, fc),
                                                           scale=col(C_HC, fc)),
                             reads=[("sc", i1), "cp"], writes=[("sc", i2)])
                        P.op("act", lambda e: e.activation(out=scr[:, i3, 0:W], in_=psb[bi][:], func=AF.Tanh, bias=col(C_HBI, fc), scale=0.5),
                             reads=[("ps", bi), "cp"], writes=[("sc", i3)])
                        P.op(PENG, lambda e: e.tensor_tensor(out=scr[:, i1, 0:W], in0=scr[:, i2, 0:W], in1=scr[:, i2, 0:W], op=ALU.mult),
                             reads=[("sc", i2), ("sc", i1)], writes=[("sc", i1)])

                    def gates_b(cc, tt):
                        i1, i2, i3 = gsc[(cc, tt)]
                        P.op("act", lambda e: e.activation(out=scr[:, i1, 0:W], in_=scr[:, i1, 0:W], func=AF.Sqrt, bias=ONE_AP, scale=-1.0),
                             reads=[("sc", i1), "cst"], writes=[("sc", i1)])

                    def gates_c(cc, tt):
                        fc = hh * 2 + cc
                        u = U(cc, tt)
                        i1, i2, i3 = gsc[(cc, tt)]
                        P.op("dve", lambda e: e.scalar_tensor_tensor(out=scr[:, i3, 0:W], in0=scr[:, i3, 0:W], scalar=1.0, in1=TA[:, u, :],
                                                                     op0=ALU.add, op1=ALU.mult),
                             reads=[("sc", i3), (nA, u)], writes=[("sc", i3)])
                        P.op("dve", lambda e: e.scalar_tensor_tensor(out=scr[:, i3, 0:W], in0=scr[:, i3, 0:W], scalar=0.5, in1=scr[:, i1, 0:W],
                                                                     op0=ALU.mult, op1=ALU.mult),
                             reads=[("sc", i3), ("sc", i1)], writes=[("sc", i3)])
                        P.op("dve", lambda e: e.tensor_tensor_scan(out=TA[:, u, :], data0=scr[:, i2, 0:W], data1=scr[:, i3, 0:W],
                                                                   initial=hst[:, li, fc:fc + 1], op0=ALU.mult, op1=ALU.add),
                             reads=[("sc", i2), ("sc", i3), ("hs", li, fc), (nA, u)], writes=[(nA, u)])
                        P.op("act", lambda e: e.activation(out=hst[:, li, fc:fc + 1], in_=TA[:, u, W - 1:W], func=AF.Copy),
                             reads=[(nA, u)], writes=[("hs", li, fc)])

                    if hh == 0:
                        s0, s5, s1 = zslot(0), zslot(5), zslot(1)
                        for tts in ((0,), (1,)):
                            zrun(s0, ev_rnnx, tts)
                            zrun(s5, ev_grnn, tts)
                            zrun(s1, ev_rnny, tts)
                    else:
                        zgroup(0, ev_rnnx)
                        zgroup(5, ev_grnn)
                        zgroup(1, ev_rnny)
                    sg = wslot([(lambda d: kview(d, 8, 256)[:, 0:2, :], wsrc(w_rg_r[L, hh], 0, 2, 0, 256)),
                                (lambda d: kview(d, 8, 256)[:, 2:4, :], wsrc(w_rg_i[L, hh], 0, 2, 0, 256))])
                    for cc, tt in units:
                        gates_a(sg, cc, tt)
                    for cc, tt in units:
                        gates_b(cc, tt)
                    for cc, tt in units:
                        gates_c(cc, tt)
                    zgroup(2, ev_convb)
                    zgroup(3, ev_convc)
                    zgroup(4, ev_convx)
                    zgroup(6, ev_gconv)

                for hh in range(4):
                    par = st["hcnt"] % 2
                    st["hcnt"] += 1
                    if par == 0:
                        do_head(hh, llA, "llA", llD, "llD")
                    else:
                        do_head(hh, llD, "llD", llA, "llA")

                oslots = [wslot([(lambda d: kview(d, 8, 256), wsrc(w_out[L], 0, 8, cs * 256, 256))]) for cs in range(4)]

                def out_part(tt, css):
                    for cs in css:
                        for cc in range(2):
                            fc = cs * 2 + cc
                            b = next_ps()
                            mm_group(oslots[cs], list(range(8)), cc * 128, lambda k: actT[:, k, tt * W:(tt + 1) * W],
                                     lambda k: [("act", k, tt)], b)
                            P.op("dve", lambda e, fc=fc, b=b: e.scalar_tensor_tensor(
                                out=xres[:, fc, tt * W:(tt + 1) * W], in0=psb[b][:], scalar=0.5, in1=xres[:, fc, tt * W:(tt + 1) * W],
                                op0=ALU.mult, op1=ALU.add), reads=[("ps", b), ("x", fc, tt)], writes=[("x", fc, tt)])

                pb = st["pbuf"]
                st["pbuf"] = 1 - pb
                for kc in range(2):
                    P.dma("pool", "p%d_%d" % (pb, kc), lambda e, kc=kc: e.dma_start(out=ptl[:, pb, kc, :], in_=pT[L, ch, kc]),
                          writes=[("p", pb, kc)])
                out_part(0, range(4))
                norm_sqs(0)
                out_part(1, (0, 1))
                nb0 = norm_mm(0)
                norm_fin(o + C_GFFN, 0, nb0)
                out_part(1, (2, 3))
                norm_sqs(1)
                nb1 = norm_mm(1)
                norm_fin(o + C_GFFN, 1, nb1)

                def up_slot(j):
                    return wslot([(lambda d: kview(d, 8, 256)[:, :, 0:128], wsrc(w_gu[L], 0, 8, j * 128, 128)),
                                  (lambda d: kview(d, 8, 256)[:, :, 128:256], wsrc(w_gu[L], 0, 8, DFF + j * 128, 128))])

                def up_unit(s, j, tt):
                    rf, rr = h_rhs(tt)
                    bg = next_ps()
                    mm_group(s, list(range(8)), 0, rf, rr, bg)
                    bu = next_ps()
                    mm_group(s, list(range(8)), 128, rf, rr, bu)
                    i = next_scr()
                    P.op("act", lambda e: e.activation(out=scr[:, i, 0:W], in_=psb[bg][:], func=AF.Tanh, bias=ZERO_AP, scale=0.5),
                         reads=[("ps", bg), "cst"], writes=[("sc", i)])
                    P.op("dve", lambda e: e.scalar_tensor_tensor(out=scr[:, i, 0:W], in0=scr[:, i, 0:W], scalar=1.0, in1=psb[bg][:],
                                                                 op0=ALU.add, op1=ALU.mult),
                         reads=[("sc", i), ("ps", bg)], writes=[("sc", i)])
                    P.op("dve", lambda e: e.tensor_tensor(out=actT[:, j, tt * W:(tt + 1) * W], in0=scr[:, i, 0:W], in1=psb[bu][:], op=ALU.mult),
                         reads=[("sc", i), ("ps", bu)], writes=[("act", j, tt)])

                JB = 4
                first_slots = [up_slot(j) for j in range(JB)]
                for tt in range(NT):
                    for j in range(JB):
                        up_unit(first_slots[j], j, tt)
                for j in range(JB, NJ):
                    s = up_slot(j)
                    for tt in range(NT):
                        up_unit(s, j, tt)
                for cs in range(4):
                    banks = {}
                    for kg in range(3):
                        k0 = kg * 8
                        nk = min(8, NJ - k0)
                        s = wslot([(lambda d, nk=nk: kview(d, 8, 256)[:, 0:nk, :], wsrc(w_dn[L], k0 * 128, nk, cs * 256, 256))])
                        for cc in range(2):
                            for tt in range(NT):
                                if kg == 0:
                                    banks[(cc, tt)] = next_ps()
                                b = banks[(cc, tt)]
                                mm_group(s, list(range(k0, k0 + nk)), cc * 128, lambda k, tt=tt: actT[:, k, tt * W:(tt + 1) * W],
                                         lambda k, tt=tt: [("act", k, tt)], b, first=(kg == 0), last=(kg == 2), koff=k0)
                    for cc in range(2):
                        fc = cs * 2 + cc
                        for tt in range(NT):
                            b = banks[(cc, tt)]
                            P.op("dve", lambda e, fc=fc, tt=tt, b=b: e.scalar_tensor_tensor(
                                out=xres[:, fc, tt * W:(tt + 1) * W], in0=psb[b][:], scalar=0.5, in1=xres[:, fc, tt * W:(tt + 1) * W],
                                op0=ALU.mult, op1=ALU.add), reads=[("ps", b), ("x", fc, tt)], writes=[("x", fc, tt)])

                norm(o + C_GPLE)
                spl = wslot([(lambda d: kview(d, 2, 1024), wsrc(w_pl[L], 0, 2, 0, 1024))])
                gslots = [wslot([(lambda d: kview(d, 8, 256), wsrc(w_pg[L], 0, 8, cs * 256, 256))]) for cs in range(4)]

                def ple_part(tt, css):
                    for cs in css:
                        for cc in range(2):
                            ple_unit(tt, cs, cc)

                def ple_unit(tt, cs, cc):
                    fc = cs * 2 + cc
                    rf, rr = h_rhs(tt)
                    bg = next_ps()
                    mm_group(gslots[cs], list(range(8)), cc * 128, rf, rr, bg)
                    bp = next_ps()
                    for kc in range(2):
                        P.op("pe", lambda e, kc=kc: e.matmul(
                            psb[bp][:], kview(ring[:, spl, :], 2, 1024)[:, kc, fc * 128:(fc + 1) * 128],
                            ptl[:, pb, kc, tt * W:(tt + 1) * W], start=(kc == 0), stop=(kc == 1)),
                            reads=[("ws", spl), ("p", pb, kc)], writes=[("ps", bp)], inc=(kc == 1))
                    i = next_scr()
                    P.op("act", lambda e: e.activation(out=scr[:, i, 0:W], in_=psb[bg][:], func=AF.Tanh, bias=ZERO_AP, scale=0.5),
                         reads=[("ps", bg), "cst"], writes=[("sc", i)])
                    P.op("dve", lambda e: e.scalar_tensor_tensor(out=scr[:, i, 0:W], in0=scr[:, i, 0:W], scalar=1.0, in1=psb[bp][:],
                                                                 op0=ALU.add, op1=ALU.mult),
                         reads=[("sc", i), ("ps", bp)], writes=[("sc", i)])
                    P.op("dve", lambda e: e.scalar_tensor_tensor(
                        out=xres[:, fc, tt * W:(tt + 1) * W], in0=scr[:, i, 0:W], scalar=0.5, in1=xres[:, fc, tt * W:(tt + 1) * W],
                        op0=ALU.mult, op1=ALU.add), reads=[("sc", i), ("x", fc, tt)], writes=[("x", fc, tt)])

                ple_part(0, range(4))
                if nxt is None:
                    ple_part(1, range(4))
                else:
                    norm_sqs(0)
                    ple_part(1, (0, 1))
                    mb0 = norm_mm(0)
                    norm_fin(nxt, 0, mb0)
                    ple_part(1, (2, 3))
                    norm_sqs(1)
                    mb1 = norm_mm(1)
                    norm_fin(nxt, 1, mb1)

            out_tokens = []
            gcol = nl * LW
            MB_AP = cpt[:, gcol + 8:gcol + 9]
            MK_AP = cpt[:, gcol + 9:gcol + 10]
            xall = lambda fc: [("x", fc, 0), ("x", fc, 1)]
            for ch in range(nch):
                for fc in range(NFC):
                    P.dma("sp", "xin%d" % (fc % 4), lambda e, fc=fc, ch=ch: e.dma_start(out=xres[:, fc, :], in_=xT[ch, fc]),
                          writes=xall(fc))
                if pipe and ch >= 1:
                    for fc in range(NFC):
                        q, r = fc // 2, fc % 2
                        for tt in range(NT):
                            i = next_scr()
                            P.dma("sp", "stg%d" % i, lambda e, q=q, r=r, tt=tt, i=i: e.dma_start(
                                out=scr[:, i, 0:W], in_=exout[q].ap()[r * 128:(r + 1) * 128, tt * W:(tt + 1) * W]),
                                  reads=[("exo", q)], writes=[("sc", i)])
                            P.op("dve", lambda e, fc=fc, tt=tt, i=i: e.scalar_tensor_tensor(
                                out=xres[:, fc, tt * W:(tt + 1) * W], in0=scr[:, i, 0:W], scalar=MB_AP,
                                in1=xres[:, fc, tt * W:(tt + 1) * W], op0=ALU.mult, op1=ALU.add),
                                 reads=[("sc", i), "cp", ("x", fc, tt)], writes=[("x", fc, tt)])
                for li, L in enumerate(layers):
                    layer(ch, li, L, li == 0, ((L + 1) * LW + C_GMIX) if li + 1 < nl else None)
                if pipe and ch == 0:
                    for (tl, key) in ((hst, "hs"), (c4h, "c4"), (c3h, "c3")):
                        P.op("dve", lambda e, tl=tl: e.tensor_scalar(out=tl[:], in0=tl[:], scalar1=MK_AP, scalar2=None, op0=ALU.mult),
                             reads=["cp"] + [(key, l, f) for l in range(nl) for f in range(NFC)],
                             writes=[(key, l, f) for l in range(nl) for f in range(NFC)])
                if pipe:
                    for q in range(4):
                        P.dma("sp", "exi%d" % q, lambda e, q=q: e.dma_start(out=exin[q].ap().rearrange("(k p) n -> p k n", p=128),
                                                                            in_=xres[:, 2 * q:2 * q + 2, :]),
                              reads=xall(2 * q) + xall(2 * q + 1), writes=[("exi", q)])
                    for q in range(4):
                        P.cc("pool", "ag%d" % q, lambda e, q=q: e.collective_compute("AllGather", ALU.bypass,
                                                                          replica_groups=[[2 * i, 2 * i + 1] for i in range(npairs)],
                                                                          ins=[exin[q].ap()], outs=[exout[q].ap()]),
                             reads=[("exi", q)], writes=[("exo", q)])
                norm(gcol, out_fn=lambda fc, ts: xres[:, fc, ts], out_res=lambda fc, tt: ("x", fc, tt))
                for fc in range(NFC):
                    out_tokens.append(P.dma("sp", "xout%d" % (fc % 4), lambda e, fc=fc, ch=ch: e.dma_start(out=oT[ch, fc], in_=xres[:, fc, :]),
                                            reads=xall(fc)))
            P.wait_all("sp", out_tokens)

        rec = []
        emit_all(DryProg(), rec, None)
        P = Prog(nc)
        emit_all(P, None, rec)
        P.emit()
    return nc


def _pack_cp(inp, layer_ids, mb, mkeep):
    nl = len(layer_ids)
    ncp = nl * LW + 16
    cp = np.zeros((128, ncp), np.float32)

    def put(col, vec):
        cp[:, col:col + 8] = np.asarray(vec, np.float32).reshape(8, 128).T

    for li, L in enumerate(layer_ids):
        o = li * LW
        put(o + C_GMIX, inp["g_mix"][L])
        put(o + C_GFFN, inp["g_ffn"][L])
        put(o + C_GPLE, inp["g_ple"][L])
        for k in range(4):
            put(o + C_C4W + 8 * k, inp["conv4_w"][L, k])
        put(o + C_C4B, inp["conv4_b"][L])
        put(o + C_BRR, np.asarray(inp["b_rg_r"][L]).reshape(-1))
        put(o + C_BRI, np.asarray(inp["b_rg_i"][L]).reshape(-1))
        put(o + C_LAM, inp["lru_lambda"][L])
        for k in range(3):
            put(o + C_C3W + 8 * k, inp["conv3_w"][L, k])
    put(nl * LW, inp["g_final"])
    cp[:, nl * LW + 8] = mb
    cp[:, nl * LW + 9] = mkeep
    return cp


_CACHE = {}
_WNAMES = ("w_in", "w_rg_r", "w_rg_i", "w_out", "w_gate_up", "w_down", "w_ple_gate", "w_ple")


def _prog(nch, nl, pipe, npairs=4):
    key = (nch, nl, pipe, npairs)
    if key not in _CACHE:
        _CACHE[key] = build_program(nch, nl, pipe, npairs)
    return _CACHE[key]


def _run_solo(inp, nseq, nch):
    nc = _prog(nch, DEPTH, False)
    ids = list(range(DEPTH))
    cp = _pack_cp(inp, ids, 0.0, 1.0)
    f32 = lambda a: np.ascontiguousarray(np.asarray(a, dtype=np.float32))
    wts = {k: f32(inp[k]) for k in _WNAMES}
    x = np.asarray(inp["x"], np.float32)
    p = np.asarray(inp["p"], np.float32)
    in_maps = []
    for b in range(nseq):
        xb = x[b, :nch * TC].reshape(nch, TC, NFC, 128).transpose(0, 2, 3, 1)
        pb = p[:, b, :nch * TC].reshape(DEPTH, nch, TC, 2, 128).transpose(0, 1, 3, 4, 2)
        m = {"xT": np.ascontiguousarray(xb), "pT": np.ascontiguousarray(pb), "cp": cp}
        m.update(wts)
        in_maps.append(m)
    res = run_bass_kernel_spmd(nc, in_maps, core_ids=list(range(nseq)))
    outs = []
    for b in range(nseq):
        o = np.asarray(res.results[b]["oT"], np.float32)
        outs.append(o.transpose(0, 3, 1, 2).reshape(nch * TC, D))
    return np.stack(outs, 0)


def _run_pipe(inp, nseq, nch):
    nit = nch + 1
    nl = DEPTH // 2
    nc = _prog(nit, nl, True, nseq)
    f32 = lambda a: np.ascontiguousarray(np.asarray(a, dtype=np.float32))
    x = np.asarray(inp["x"], np.float32)
    p = np.asarray(inp["p"], np.float32)
    in_maps = []
    for b in range(nseq):
        for half in range(2):
            ids = [half * nl + i for i in range(nl)]
            xT = np.zeros((nit, NFC, 128, TC), np.float32)
            pT = np.zeros((nl, nit, 2, 128, TC), np.float32)
            pb = p[ids][:, b, :nch * TC].reshape(nl, nch, TC, 2, 128).transpose(0, 1, 3, 4, 2)
            if half == 0:
                xT[:nch] = x[b, :nch * TC].reshape(nch, TC, NFC, 128).transpose(0, 2, 3, 1)
                pT[:, :nch] = pb
            else:
                pT[:, 1:] = pb
            m = {"xT": xT, "pT": pT, "cp": _pack_cp(inp, ids, float(half), float(1 - half))}
            for k in _WNAMES:
                m[k] = f32(np.asarray(inp[k])[ids])
            in_maps.append(m)
    res = run_bass_kernel_spmd(nc, in_maps, core_ids=list(range(2 * nseq)))
    _CACHE["last"] = res
    outs = []
    for b in range(nseq):
        o = np.asarray(res.results[2 * b + 1]["oT"], np.float32)[1:]
        outs.append(o.transpose(0, 3, 1, 2).reshape(nch * TC, D))
    return np.stack(outs, 0)


def kernel(**inputs):
    out = _run_pipe(inputs, BATCH, SEQ // TC)
    return np.ascontiguousarray(out.astype(np.float32))
```

```python
import math
from contextlib import ExitStack

import numpy as np
import concourse.bass as bass
import concourse.mybir as mybir
from concourse.bass_utils import run_bass_kernel_spmd

F32 = mybir.dt.float32
BF16 = mybir.dt.bfloat16
AF = mybir.ActivationFunctionType
ALU = mybir.AluOpType

D = 1024
NFC = 8
DFF = 2816
NJ = 22
PLE = 256
DEPTH = 4
SEQ = 8192
BATCH = 4
TC = 1024
W = 512
NT = TC // W
EPS = 1e-6
NSLOT = 8
NSCR = 12
import os
PENG = os.environ.get('PENG', 'dve')
DEFER = int(os.environ.get('DEFER', '15'))
KAHEAD = int(os.environ.get('KAHEAD', '2'))

C_GMIX, C_GFFN, C_GPLE, C_C4W, C_C4B, C_BRR, C_BRI, C_LAM, C_C3W = 0, 8, 16, 24, 56, 64, 72, 80, 88
C_HC, C_HBR, C_HBI, C_C3W2, C_TMP = 112, 120, 128, 136, 160
LW = 168
GELU_K0 = math.sqrt(2.0 / math.pi)
GELU_K1 = GELU_K0 * 0.044715


class Prog:
    ENGS = ("pe", "act", "dve", "pool", "sp")

    def __init__(self, nc):
        self.nc = nc
        self.ops = {e: [] for e in self.ENGS}
        self.cnt = {}
        self.seen = {}
        self.last_w = {}
        self.readers = {}
        self.semkeys = []
        for e in self.ENGS[:4]:
            self._sem(e)

    def _sem(self, key):
        if key not in self.cnt:
            self.cnt[key] = 0
            self.semkeys.append(key)

    def _deps(self, reads, writes):
        deps = []
        for r in reads:
            t = self.last_w.get(r)
            if t is not None:
                deps.append(t)
        for w in writes:
            t = self.last_w.get(w)
            if t is not None:
                deps.append(t)
            deps.extend(self.readers.get(w, ()))
        return deps

    def _commit(self, token, reads, writes):
        for r in reads:
            self.readers.setdefault(r, []).append(token)
        for w in writes:
            self.last_w[w] = token
            self.readers[w] = []

    def _waits(self, eng, deps, token):
        waits = {}
        for (k, v) in deps:
            if (k, v) == token:
                continue
            if k == "pe" and eng == "pe":
                continue
            if self.seen.get((eng, k), 0) >= v:
                continue
            if waits.get(k, 0) < v:
                waits[k] = v
        for k, v in waits.items():
            self.seen[(eng, k)] = v
        return list(waits.items())

    def op(self, eng, fn, reads=(), writes=(), inc=True):
        deps = self._deps(reads, writes)
        if inc:
            self.cnt[eng] += 1
            token = (eng, self.cnt[eng])
        else:
            token = (eng, self.cnt[eng] + 1)
        waits = self._waits(eng, deps, token)
        self.ops[eng].append((waits, fn, (eng, 1) if inc else None))
        self._commit(token, reads, writes)
        return token

    def dma(self, queue, chan, fn, reads=(), writes=()):
        key = ("dma", chan)
        self._sem(key)
        deps = self._deps(reads, writes)
        if self.cnt[key] > 0:
            deps.append((key, self.cnt[key]))
        self.cnt[key] += 16
        token = (key, self.cnt[key])
        waits = self._waits(queue, deps, token)
        self.ops[queue].append((waits, fn, (key, 16)))
        self._commit(token, reads, writes)
        return token

    def cc(self, queue, chan, fn, reads=(), writes=()):
        key = ("cc", chan)
        self._sem(key)
        deps = self._deps(reads, writes)
        if self.cnt[key] > 0:
            deps.append((key, self.cnt[key]))
        self.cnt[key] += 1
        token = (key, self.cnt[key])
        waits = self._waits(queue, deps, token)
        self.ops[queue].append((waits, fn, (key, 1)))
        self._commit(token, reads, writes)
        return token

    def wait_all(self, eng, tokens):
        waits = self._waits(eng, tokens, None)
        self.ops[eng].append((waits, None, None))

    def emit(self):
        nc = self.nc
        with ExitStack() as es:
            sems = {}
            for i, k in enumerate(self.semkeys):
                sems[k] = es.enter_context(nc.semaphore("s%d" % i))
            block = es.enter_context(nc.Block())

            def run(engname):
                def body(eng):
                    for waits, fn, inc in self.ops[engname]:
                        for k, v in waits:
                            eng.wait_ge(sems[k], v)
                        if fn is not None:
                            ins = fn(eng)
                            if inc is not None:
                                ins.then_inc(sems[inc[0]], inc[1])
                return body

            block.tensor(run("pe"))
            block.scalar(run("act"))
            block.vector(run("dve"))
            block.gpsimd(run("pool"))
            block.sync(run("sp"))


class DryProg:
    def op(self, *a, **k):
        return None

    def dma(self, *a, **k):
        return None

    def cc(self, *a, **k):
        return None

    def wait_all(self, *a, **k):
        pass


def build_program(nch, nl, pipe, npairs=4):
    layers = list(range(nl))
    ncp = nl * LW + 16
    nc = bass.Bass("TRN2", target_bir_lowering=False)
    dt = nc.dram_tensor
    xT = dt("xT", [nch, NFC, 128, TC], F32, kind="ExternalInput").ap()
    pT = dt("pT", [nl, nch, 2, 128, TC], F32, kind="ExternalInput").ap()
    cp = dt("cp", [128, ncp], F32, kind="ExternalInput").ap()
    w_in = dt("w_in", [nl, D, 7 * D], F32, kind="ExternalInput").ap()
    w_rg_r = dt("w_rg_r", [nl, 4, 256, 256], F32, kind="ExternalInput").ap()
    w_rg_i = dt("w_rg_i", [nl, 4, 256, 256], F32, kind="ExternalInput").ap()
    w_out = dt("w_out", [nl, D, D], F32, kind="ExternalInput").ap()
    w_gu = dt("w_gate_up", [nl, D, 2 * DFF], F32, kind="ExternalInput").ap()
    w_dn = dt("w_down", [nl, DFF, D], F32, kind="ExternalInput").ap()
    w_pg = dt("w_ple_gate", [nl, D, D], F32, kind="ExternalInput").ap()
    w_pl = dt("w_ple", [nl, PLE, D], F32, kind="ExternalInput").ap()
    oT = dt("oT", [nch, NFC, 128, TC], F32, kind="ExternalOutput").ap()
    if pipe:
        exin = [nc.dram_tensor("exin%d" % q, [256, TC], F32) for q in range(4)]
        exout = [nc.dram_tensor("exout%d" % q, [512, TC], F32) for q in range(4)]

    es = ExitStack()
    with es:
        def sb(name, shape, dtype):
            return es.enter_context(nc.sbuf_tensor(name, shape, dtype))

        xres = sb("xres", [128, NFC, TC], F32)
        hT = sb("hT", [128, NFC, TC], BF16)
        sq = sb("sq", [128, NFC, W], BF16)
        actT = sb("actT", [128, NJ, TC], BF16)
        ptl = sb("ptl", [128, 2, 2, TC], BF16)
        ring = sb("ring", [128, NSLOT, 8 * 256], BF16)
        cpt = sb("cpt", [128, ncp], F32)
        cst = sb("cst", [128, 4], F32)
        ones = sb("ones", [128, 128], BF16)
        scr = sb("scr", [128, NSCR, 516], F32)
        llA = sb("llA", [128, 4, W], F32)
        llB = sb("llB", [128, 4, W], F32)
        llC = sb("llC", [128, 4, W], F32)
        llD = sb("llD", [128, 4, W], F32)
        xcb = sb("xcb", [128, 4, W], BF16)
        rstd = sb("rstd", [128, 2, W], F32)
        hst = sb("hst", [128, nl, NFC], F32)
        c4h = sb("c4h", [128, nl, NFC, 3], F32)
        c3h = sb("c3h", [128, nl, NFC, 2], F32)
        psb = [es.enter_context(nc.psum_tensor("ps%d" % i, [128, W], F32)) for i in range(8)]

        def emit_all(P, rec, specs):
            st = {"ps": 0, "scr": 0, "ws": 0, "pbuf": 0, "hcnt": 0, "wn": 0, "wi": 0}

            def next_ps():
                b = st["ps"]
                st["ps"] = (b + 1) % 8
                return b

            def next_scr():
                i = st["scr"]
                st["scr"] = (i + 1) % NSCR
                return i

            EPS_AP = cst[:, 0:1]
            ONE_AP = cst[:, 1:2]
            ZERO_AP = cst[:, 2:3]

            P.op("dve", lambda e: e.memset(cst[:, 0:1], EPS), writes=["cst"])
            P.op("dve", lambda e: e.memset(cst[:, 1:2], 1.0), writes=["cst"])
            P.op("dve", lambda e: e.memset(cst[:, 2:4], 0.0), writes=["cst"])
            P.op("dve", lambda e: e.memset(ones[:], 1.0 / D), writes=["ones"])
            P.op("dve", lambda e: e.memset(hst[:], 0.0), writes=[("hs", l, f) for l in range(nl) for f in range(NFC)])
            P.op("dve", lambda e: e.memset(c4h[:], 0.0), writes=[("c4", l, f) for l in range(nl) for f in range(NFC)])
            P.op("dve", lambda e: e.memset(c3h[:], 0.0), writes=[("c3", l, f) for l in range(nl) for f in range(NFC)])
            P.dma("sp", "cp", lambda e: e.dma_start(out=cpt[:], in_=cp), writes=["cp"])
            for li, L in enumerate(layers):
                o = L * LW
                P.op("act", lambda e, o=o: e.activation(out=cpt[:, o + C_TMP:o + C_TMP + 8], in_=cpt[:, o + C_LAM:o + C_LAM + 8],
                                                          func=AF.Exp, bias=ZERO_AP, scale=-1.0), reads=["cp", "cst"], writes=["cp"])
                P.op("act", lambda e, o=o: e.activation(out=cpt[:, o + C_TMP:o + C_TMP + 8], in_=cpt[:, o + C_TMP:o + C_TMP + 8],
                                                          func=AF.Ln, bias=ONE_AP, scale=1.0), reads=["cp", "cst"], writes=["cp"])
                P.op("dve", lambda e, o=o: e.tensor_scalar(out=cpt[:, o + C_HC:o + C_HC + 8], in0=cpt[:, o + C_TMP:o + C_TMP + 8],
                                                             scalar1=-4.0, scalar2=None, op0=ALU.mult), reads=["cp"], writes=["cp"])
                P.op("dve", lambda e, o=o: e.tensor_scalar(out=cpt[:, o + C_HBR:o + C_HBR + 16], in0=cpt[:, o + C_BRR:o + C_BRR + 16],
                                                             scalar1=0.5, scalar2=None, op0=ALU.mult), reads=["cp"], writes=["cp"])
                P.op("dve", lambda e, o=o: e.tensor_scalar(out=cpt[:, o + C_C3W2:o + C_C3W2 + 24], in0=cpt[:, o + C_C3W:o + C_C3W + 24],
                                                             scalar1=2.0, scalar2=None, op0=ALU.mult), reads=["cp"], writes=["cp"])

            def wslot(dmas):
                if rec is not None:
                    rec.append(dmas)
                    s = st["ws"]
                    st["ws"] = (s + 1) % NSLOT
                    return s
                n = st["wn"]
                st["wn"] += 1
                while st["wi"] <= min(n + KAHEAD, len(specs) - 1):
                    m = st["wi"]
                    st["wi"] += 1
                    sl = m % NSLOT
                    for dst_fn, src in specs[m]:
                        P.dma("pool", "w%d" % sl, lambda e, d=dst_fn(ring[:, sl, :]), src=src: e.dma_start(out=d, in_=src),
                              writes=[("ws", sl)])
                return n % NSLOT

            def kview(ap, nk, ncol):
                return ap[:, 0:nk * ncol].rearrange("p (k n) -> p k n", k=nk)

            def wsrc(mat, r0, nk, c0, ncol):
                return mat[r0:r0 + nk * 128, c0:c0 + ncol].rearrange("(k p) n -> p k n", p=128)

            def mm_group(s, kcs, col0, rhs_fn, rhs_res, bank, first=True, last=True, koff=0):
                n = len(kcs)
                for i, k in enumerate(kcs):
                    P.op("pe", lambda e, k=k, i=i: e.matmul(psb[bank][:], kview(ring[:, s, :], 8, 256)[:, k - koff, col0:col0 + 128],
                                                             rhs_fn(k), start=(first and i == 0), stop=(last and i == n - 1)),
                         reads=[("ws", s)] + rhs_res(k), writes=[("ps", bank)], inc=(i == n - 1))

            def norm_sqs(tt):
                ts = slice(tt * W, (tt + 1) * W)
                for fc in range(NFC):
                    if fc % 2 == 0:
                        P.op("act", lambda e, fc=fc: e.activation(out=sq[:, fc, :], in_=xres[:, fc, ts], func=AF.Square,
                                                                  bias=ZERO_AP, scale=1.0),
                             reads=[("x", fc, tt), "cst"], writes=[("sq", fc)])
                    else:
                        P.op("dve", lambda e, fc=fc: e.tensor_tensor(out=sq[:, fc, :], in0=xres[:, fc, ts], in1=xres[:, fc, ts], op=ALU.mult),
                             reads=[("x", fc, tt)], writes=[("sq", fc)])

            def norm_mm(tt):
                b = next_ps()
                for fc in range(NFC):
                    P.op("pe", lambda e, fc=fc: e.matmul(psb[b][:], ones[:], sq[:, fc, :], start=(fc == 0), stop=(fc == NFC - 1)),
                         reads=["ones", ("sq", fc)], writes=[("ps", b)], inc=(fc == NFC - 1))
                return b

            def norm_fin(gcol, tt, b, out_fn=None, out_res=None):
                ts = slice(tt * W, (tt + 1) * W)
                P.op("act", lambda e: e.activation(out=rstd[:, tt, :], in_=psb[b][:], func=AF.Sqrt, bias=EPS_AP, scale=1.0),
                     reads=[("ps", b), "cst"], writes=[("rstd", tt)])
                P.op("dve", lambda e: e.reciprocal(out=rstd[:, tt, :], in_=rstd[:, tt, :]), reads=[("rstd", tt)], writes=[("rstd", tt)])
                for fc in range(NFC):
                    dst = hT[:, fc, ts] if out_fn is None else out_fn(fc, ts)
                    dres = ("h", fc, tt) if out_fn is None else out_res(fc, tt)
                    P.op("dve", lambda e, fc=fc, dst=dst: e.scalar_tensor_tensor(out=dst, in0=xres[:, fc, ts],
                                                                                 scalar=cpt[:, gcol + fc:gcol + fc + 1], in1=rstd[:, tt, :],
                                                                                 op0=ALU.mult, op1=ALU.mult),
                         reads=[("x", fc, tt), ("rstd", tt), "cp"], writes=[dres])

            def norm(gcol, out_fn=None, out_res=None):
                for tt in range(NT):
                    norm_sqs(tt)
                    b = norm_mm(tt)
                    norm_fin(gcol, tt, b, out_fn, out_res)

            def h_rhs(tt):
                return (lambda k: hT[:, k, tt * W:(tt + 1) * W]), (lambda k: [("h", k, tt)])

            def layer(ch, li, L, first, nxt):
                o = L * LW
                col = lambda c, fc: cpt[:, o + c + fc:o + c + fc + 1]
                if first:
                    norm(o + C_GMIX)
                def do_head(hh, TA, nA, TD, nD):
                    units = [(cc, tt) for tt in range(NT) for cc in range(2)]
                    U = lambda cc, tt: cc * NT + tt
                    TB, nB, TC, nC = llB, "llB", llC, "llC"

                    def zslot(g):
                        return wslot([(lambda d: kview(d, 8, 256), wsrc(w_in[L], 0, 8, g * D + hh * 256, 256))])

                    def zrun(s, evac, tts):
                        pend = None
                        for tt in tts:
                            for cc in range(2):
                                b = next_ps()
                                rf, rr = h_rhs(tt)
                                mm_group(s, list(range(8)), cc * 128, rf, rr, b)
                                d = evac(cc, tt, b)
                                if pend is not None:
                                    pend()
                                pend = d
                        if pend is not None:
                            pend()

                    def zgroup(g, evac):
                        zrun(zslot(g), evac, range(NT))

                    def ev_rnnx(cc, tt, b):
                        fc = hh * 2 + cc
                        u = U(cc, tt)
                        i = next_scr()
                        P.op("act", lambda e: e.activation(out=scr[:, i, 0:3], in_=c4h[:, li, fc, :], func=AF.Copy),
                             reads=[("c4", li, fc)], writes=[("sch", i), ("sc", i)])
                        P.op("act", lambda e: e.activation(out=scr[:, i, 3:3 + W], in_=psb[b][:], func=AF.Copy),
                             reads=[("ps", b)], writes=[("sc", i)])
                        P.op("act", lambda e: e.activation(out=TA[:, u, :], in_=psb[b][:], func=AF.Identity,
                                                           bias=col(C_C4B, fc), scale=col(C_C4W + 24, fc)),
                             reads=[("ps", b), "cp"], writes=[(nA, u)])
                        P.op("act", lambda e: e.activation(out=c4h[:, li, fc, :], in_=scr[:, i, W:W + 3], func=AF.Copy),
                             reads=[("sc", i)], writes=[("c4", li, fc)])
                        def later():
                            for k in (2, 1, 0):
                                P.op("dve", lambda e, k=k: e.scalar_tensor_tensor(out=TA[:, u, :], in0=scr[:, i, k:k + W],
                                                                                  scalar=col(C_C4W + 8 * k, fc), in1=TA[:, u, :],
                                                                                  op0=ALU.mult, op1=ALU.add),
                                     reads=[("sc", i), ("sch", i), (nA, u), "cp"], writes=[(nA, u)])
                            P.op("act", lambda e: e.activation(out=xcb[:, u, :], in_=TA[:, u, :], func=AF.Copy),
                                 reads=[(nA, u)], writes=[("xcb", u)])
                        if DEFER & 1:
                            return later
                        later()
                        return None

                    def ev_grnn(cc, tt, b):
                        u = U(cc, tt)
                        P.op("act", lambda e: e.activation(out=TD[:, u, :], in_=psb[b][:], func=AF.Tanh, bias=ZERO_AP, scale=0.5),
                             reads=[("ps", b), "cst"], writes=[(nD, u)])

                    def ev_rnny(cc, tt, b):
                        u = U(cc, tt)
                        P.op("act", lambda e: e.activation(out=TC[:, u, :], in_=psb[b][:], func=AF.Gelu_apprx_tanh, bias=ZERO_AP, scale=1.0),
                             reads=[("ps", b), "cst"], writes=[(nC, u)])
                        later = lambda: P.op("dve", lambda e: e.scalar_tensor_tensor(out=TC[:, u, :], in0=TD[:, u, :], scalar=1.0, in1=TC[:, u, :],
                                                                                     op0=ALU.add, op1=ALU.mult),
                                             reads=[(nD, u), (nC, u)], writes=[(nC, u)])
                        if DEFER & 2:
                            return later
                        later()
                        return None

                    def ev_convb(cc, tt, b):
                        u = U(cc, tt)
                        P.op("act", lambda e: e.activation(out=TB[:, u, :], in_=psb[b][:], func=AF.Copy),
                             reads=[("ps", b)], writes=[(nB, u)])

                    def ev_convc(cc, tt, b):
                        u = U(cc, tt)
                        P.op("act", lambda e: e.activation(out=TD[:, u, :], in_=psb[b][:], func=AF.Copy),
                             reads=[("ps", b)], writes=[(nD, u)])

                    def ev_convx(cc, tt, b):
                        fc = hh * 2 + cc
                        j = next_scr()
                        u = U(cc, tt)
                        P.op("act", lambda e: e.activation(out=scr[:, j, 0:2], in_=c3h[:, li, fc, :], func=AF.Copy),
                             reads=[("c3", li, fc)], writes=[("sch", j), ("sc", j)])
                        P.op("dve", lambda e: e.tensor_tensor(out=scr[:, j, 2:2 + W], in0=TD[:, u, :], in1=psb[b][:], op=ALU.mult),
                             reads=[(nD, u), ("ps", b)], writes=[("sc", j)])
                        def later():
                            P.op("act", lambda e: e.activation(out=c3h[:, li, fc, :], in_=scr[:, j, W:W + 2], func=AF.Copy),
                                 reads=[("sc", j)], writes=[("c3", li, fc)])
                            P.op("act", lambda e: e.activation(out=TD[:, u, :], in_=scr[:, j, 2:2 + W], func=AF.Identity,
                                                               bias=ZERO_AP, scale=col(C_C3W + 16, fc)),
                                 reads=[("sc", j), "cp", "cst"], writes=[(nD, u)])
                            for k in (1, 0):
                                P.op("dve", lambda e, k=k: e.scalar_tensor_tensor(out=TD[:, u, :], in0=scr[:, j, k:k + W],
                                                                                  scalar=col(C_C3W + 8 * k, fc), in1=TD[:, u, :],
                                                                                  op0=ALU.mult, op1=ALU.add),
                                     reads=[("sc", j), ("sch", j), (nD, u), "cp"], writes=[(nD, u)])
                            P.op(PENG, lambda e: e.tensor_tensor(out=TB[:, u, :], in0=TD[:, u, :], in1=TB[:, u, :], op=ALU.mult),
                                 reads=[(nD, u), (nB, u)], writes=[(nB, u)])
                        if DEFER & 4:
                            return later
                        later()
                        return None

                    def ev_gconv(cc, tt, b):
                        fc = hh * 2 + cc
                        u = U(cc, tt)
                        i = next_scr()
                        P.op("act", lambda e: e.activation(out=scr[:, i, 0:W], in_=psb[b][:], func=AF.Tanh, bias=ZERO_AP, scale=0.5),
                             reads=[("ps", b), "cst"], writes=[("sc", i)])
                        def later():
                            P.op("dve", lambda e: e.scalar_tensor_tensor(out=TB[:, u, :], in0=scr[:, i, 0:W], scalar=1.0, in1=TB[:, u, :],
                                                                         op0=ALU.add, op1=ALU.mult),
                                 reads=[("sc", i), (nB, u)], writes=[(nB, u)])
                            P.op(PENG, lambda e: e.tensor_tensor(out=TA[:, u, :], in0=TA[:, u, :], in1=TC[:, u, :], op=ALU.mult),
                                 reads=[(nA, u), (nC, u)], writes=[(nA, u)])
                            P.op(PENG, lambda e: e.tensor_tensor(out=actT[:, fc, tt * W:(tt + 1) * W], in0=TA[:, u, :], in1=TB[:, u, :], op=ALU.add),
                                 reads=[(nA, u), (nB, u)], writes=[("act", fc, tt)])
                        if DEFER & 8:
                            return later
                        later()
                        return None

                    gsc = {}

                    def gates_a(s, cc, tt):
                        fc = hh * 2 + cc
                        xrf = lambda k: xcb[:, (k % 2) * NT + tt, :]
                        xrr = lambda k: [("xcb", (k % 2) * NT + tt)]
                        br = next_ps()
                        mm_group(s, [0, 1], cc * 128, xrf, xrr, br)
                        bi = next_ps()
                        mm_group(s, [2, 3], cc * 128, xrf, xrr, bi)
                        i1, i2, i3 = next_scr(), next_scr(), next_scr()
                        gsc[(cc, tt)] = (i1, i2, i3)
                        P.op("act", lambda e: e.activation(out=scr[:, i1, 0:W], in_=psb[br][:], func=AF.Tanh, bias=col(C_HBR, fc), scale=0.5),
                             reads=[("ps", br), "cp"], writes=[("sc", i1)])
                        P.op("act", lambda e: e.activation(out=scr[:, i2, 0:W], in_=scr[:, i1, 0:W], func=AF.Exp, bias=col(C_HC, fc),
                                                           scale=col(C_HC, fc)),
                             reads=[("sc", i1), "cp"], writes=[("sc", i2)])
                        P.op("act", lambda e: e.activation(out=scr[:, i3, 0:W], in_=psb[bi][:], func=AF.Tanh, bias=col(C_HBI, fc), scale=0.5),
                             reads=[("ps", bi), "cp"], writes=[("sc", i3)])
                        P.op(PENG, lambda e: e.tensor_tensor(out=scr[:, i1, 0:W], in0=scr[:, i2, 0:W], in1=scr[:, i2, 0:W], op=ALU.mult),
                             reads=[("sc", i2), ("sc", i1)], writes=[("sc", i1)])

                    def gates_b(cc, tt):
                        i1, i2, i3 = gsc[(cc, tt)]
                        P.op("act", lambda e: e.activation(out=scr[:, i1, 0:W], in_=scr[:, i1, 0:W], func=AF.Sqrt, bias=ONE_AP, scale=-1.0),
                             reads=[("sc", i1), "cst"], writes=[("sc", i1)])

                    def gates_c(cc, tt):
                        fc = hh * 2 + cc
                        u = U(cc, tt)
                        i1, i2, i3 = gsc[(cc, tt)]
                        P.op("dve", lambda e: e.scalar_tensor_tensor(out=scr[:, i3, 0:W], in0=scr[:, i3, 0:W], scalar=1.0, in1=TA[:, u, :],
                                                                     op0=ALU.add, op1=ALU.mult),
                             reads=[("sc", i3), (nA, u)], writes=[("sc", i3)])
                        P.op("dve", lambda e: e.scalar_tensor_tensor(out=scr[:, i3, 0:W], in0=scr[:, i3, 0:W], scalar=0.5, in1=scr[:, i1, 0:W],
                                                                     op0=ALU.mult, op1=ALU.mult),
                             reads=[("sc", i3), ("sc", i1)], writes=[("sc", i3)])
                        P.op("dve", lambda e: e.tensor_tensor_scan(out=TA[:, u, :], data0=scr[:, i2, 0:W], data1=scr[:, i3, 0:W],
                                                                   initial=hst[:, li, fc:fc + 1], op0=ALU.mult, op1=ALU.add),
                             reads=[("sc", i2), ("sc", i3), ("hs", li, fc), (nA, u)], writes=[(nA, u)])
                        P.op("dve", lambda e: e.tensor_copy(out=hst[:, li, fc:fc + 1], in_=TA[:, u, W - 1:W]),
                             reads=[(nA, u)], writes=[("hs", li, fc)])

                    if hh == 0:
                        s0, s5, s1 = zslot(0), zslot(5), zslot(1)
                        for tts in ((0,), (1,)):
                            zrun(s0, ev_rnnx, tts)
                            zrun(s5, ev_grnn, tts)
                            zrun(s1, ev_rnny, tts)
                    else:
                        zgroup(0, ev_rnnx)
                        zgroup(5, ev_grnn)
                        zgroup(1, ev_rnny)
                    sg = wslot([(lambda d: kview(d, 8, 256)[:, 0:2, :], wsrc(w_rg_r[L, hh], 0, 2, 0, 256)),
                                (lambda d: kview(d, 8, 256)[:, 2:4, :], wsrc(w_rg_i[L, hh], 0, 2, 0, 256))])
                    for cc, tt in units:
                        gates_a(sg, cc, tt)
                    for cc, tt in units:
                        gates_b(cc, tt)
                    for cc, tt in units:
                        gates_c(cc, tt)
                    zgroup(2, ev_convb)
                    zgroup(3, ev_convc)
                    zgroup(4, ev_convx)
                    zgroup(6, ev_gconv)

                for hh in range(4):
                    par = st["hcnt"] % 2
                    st["hcnt"] += 1
                    if par == 0:
                        do_head(hh, llA, "llA", llD, "llD")
                    else:
                        do_head(hh, llD, "llD", llA, "llA")

                oslots = [wslot([(lambda d: kview(d, 8, 256), wsrc(w_out[L], 0, 8, cs * 256, 256))]) for cs in range(4)]

                def out_part(tt, css):
                    for cs in css:
                        for cc in range(2):
                            fc = cs * 2 + cc
                            b = next_ps()
                            mm_group(oslots[cs], list(range(8)), cc * 128, lambda k: actT[:, k, tt * W:(tt + 1) * W],
                                     lambda k: [("act", k, tt)], b)
                            P.op("dve", lambda e, fc=fc, b=b: e.scalar_tensor_tensor(
                                out=xres[:, fc, tt * W:(tt + 1) * W], in0=psb[b][:], scalar=0.5, in1=xres[:, fc, tt * W:(tt + 1) * W],
                                op0=ALU.mult, op1=ALU.add), reads=[("ps", b), ("x", fc, tt)], writes=[("x", fc, tt)])

                pb = st["pbuf"]
                st["pbuf"] = 1 - pb
                for kc in range(2):
                    P.dma("pool", "p%d_%d" % (pb, kc), lambda e, kc=kc: e.dma_start(out=ptl[:, pb, kc, :], in_=pT[L, ch, kc]),
                          writes=[("p", pb, kc)])
                out_part(0, range(4))
                norm_sqs(0)
                out_part(1, (0, 1))
                nb0 = norm_mm(0)
                norm_fin(o + C_GFFN, 0, nb0)
                out_part(1, (2, 3))
                norm_sqs(1)
                nb1 = norm_mm(1)
                norm_fin(o + C_GFFN, 1, nb1)

                def up_slot(j):
                    return wslot([(lambda d: kview(d, 8, 256)[:, :, 0:128], wsrc(w_gu[L], 0, 8, j * 128, 128)),
                                  (lambda d: kview(d, 8, 256)[:, :, 128:256], wsrc(w_gu[L], 0, 8, DFF + j * 128, 128))])

                def up_unit(s, j, tt):
                    rf, rr = h_rhs(tt)
                    bg = next_ps()
                    mm_group(s, list(range(8)), 0, rf, rr, bg)
                    bu = next_ps()
                    mm_group(s, list(range(8)), 128, rf, rr, bu)
                    i = next_scr()
                    P.op("act", lambda e: e.activation(out=scr[:, i, 0:W], in_=psb[bg][:], func=AF.Tanh, bias=ZERO_AP, scale=0.5),
                         reads=[("ps", bg), "cst"], writes=[("sc", i)])
                    P.op("dve", lambda e: e.scalar_tensor_tensor(out=scr[:, i, 0:W], in0=scr[:, i, 0:W], scalar=1.0, in1=psb[bg][:],
                                                                 op0=ALU.add, op1=ALU.mult),
                         reads=[("sc", i), ("ps", bg)], writes=[("sc", i)])
                    P.op("dve", lambda e: e.tensor_tensor(out=actT[:, j, tt * W:(tt + 1) * W], in0=scr[:, i, 0:W], in1=psb[bu][:], op=ALU.mult),
                         reads=[("sc", i), ("ps", bu)], writes=[("act", j, tt)])

                JB = 4
                first_slots = [up_slot(j) for j in range(JB)]
                for tt in range(NT):
                    for j in range(JB):
                        up_unit(first_slots[j], j, tt)
                for j in range(JB, NJ):
                    s = up_slot(j)
                    for tt in range(NT):
                        up_unit(s, j, tt)
                for cs in range(4):
                    banks = {}
                    for kg in range(3):
                        k0 = kg * 8
                        nk = min(8, NJ - k0)
                        s = wslot([(lambda d, nk=nk: kview(d, 8, 256)[:, 0:nk, :], wsrc(w_dn[L], k0 * 128, nk, cs * 256, 256))])
                        for cc in range(2):
                            for tt in range(NT):
                                if kg == 0:
                                    banks[(cc, tt)] = next_ps()
                                b = banks[(cc, tt)]
                                mm_group(s, list(range(k0, k0 + nk)), cc * 128, lambda k, tt=tt: actT[:, k, tt * W:(tt + 1) * W],
                                         lambda k, tt=tt: [("act", k, tt)], b, first=(kg == 0), last=(kg == 2), koff=k0)
                    for cc in range(2):
                        fc = cs * 2 + cc
                        for tt in range(NT):
                            b = banks[(cc, tt)]
                            P.op("dve", lambda e, fc=fc, tt=tt, b=b: e.scalar_tensor_tensor(
                                out=xres[:, fc, tt * W:(tt + 1) * W], in0=psb[b][:], scalar=0.5, in1=xres[:, fc, tt * W:(tt + 1) * W],
                                op0=ALU.mult, op1=ALU.add), reads=[("ps", b), ("x", fc, tt)], writes=[("x", fc, tt)])

                norm(o + C_GPLE)
                spl = wslot([(lambda d: kview(d, 2, 1024), wsrc(w_pl[L], 0, 2, 0, 1024))])
                gslots = [wslot([(lambda d: kview(d, 8, 256), wsrc(w_pg[L], 0, 8, cs * 256, 256))]) for cs in range(4)]

                def ple_part(tt, css):
                    for cs in css:
                        for cc in range(2):
                            ple_unit(tt, cs, cc)

                def ple_unit(tt, cs, cc):
                    fc = cs * 2 + cc
                    rf, rr = h_rhs(tt)
                    bg = next_ps()
                    mm_group(gslots[cs], list(range(8)), cc * 128, rf, rr, bg)
                    bp = next_ps()
                    for kc in range(2):
                        P.op("pe", lambda e, kc=kc: e.matmul(
                            psb[bp][:], kview(ring[:, spl, :], 2, 1024)[:, kc, fc * 128:(fc + 1) * 128],
                            ptl[:, pb, kc, tt * W:(tt + 1) * W], start=(kc == 0), stop=(kc == 1)),
                            reads=[("ws", spl), ("p", pb, kc)], writes=[("ps", bp)], inc=(kc == 1))
                    i = next_scr()
                    P.op("act", lambda e: e.activation(out=scr[:, i, 0:W], in_=psb[bg][:], func=AF.Tanh, bias=ZERO_AP, scale=0.5),
                         reads=[("ps", bg), "cst"], writes=[("sc", i)])
                    P.op("dve", lambda e: e.scalar_tensor_tensor(out=scr[:, i, 0:W], in0=scr[:, i, 0:W], scalar=1.0, in1=psb[bp][:],
                                                                 op0=ALU.add, op1=ALU.mult),
                         reads=[("sc", i), ("ps", bp)], writes=[("sc", i)])
                    P.op("dve", lambda e: e.scalar_tensor_tensor(
                        out=xres[:, fc, tt * W:(tt + 1) * W], in0=scr[:, i, 0:W], scalar=0.5, in1=xres[:, fc, tt * W:(tt + 1) * W],
                        op0=ALU.mult, op1=ALU.add), reads=[("sc", i), ("x", fc, tt)], writes=[("x", fc, tt)])

                ple_part(0, range(4))
                if nxt is None:
                    ple_part(1, range(4))
                else:
                    norm_sqs(0)
                    ple_part(1, (0, 1))
                    mb0 = norm_mm(0)
                    norm_fin(nxt, 0, mb0)
                    ple_part(1, (2, 3))
                    norm_sqs(1)
                    mb1 = norm_mm(1)
                    norm_fin(nxt, 1, mb1)

            out_tokens = []
            gcol = nl * LW
            MB_AP = cpt[:, gcol + 8:gcol + 9]
            MK_AP = cpt[:, gcol + 9:gcol + 10]
            xall = lambda fc: [("x", fc, 0), ("x", fc, 1)]
            for ch in range(nch):
                for fc in range(NFC):
                    P.dma("sp", "xin%d" % (fc % 4), lambda e, fc=fc, ch=ch: e.dma_start(out=xres[:, fc, :], in_=xT[ch, fc]),
                          writes=xall(fc))
                if pipe and ch >= 1:
                    for fc in range(NFC):
                        q, r = fc // 2, fc % 2
                        for tt in range(NT):
                            i = next_scr()
                            P.dma("sp", "stg%d" % i, lambda e, q=q, r=r, tt=tt, i=i: e.dma_start(
                                out=scr[:, i, 0:W], in_=exout[q].ap()[r * 128:(r + 1) * 128, tt * W:(tt + 1) * W]),
                                  reads=[("exo", q)], writes=[("sc", i)])
                            P.op("dve", lambda e, fc=fc, tt=tt, i=i: e.scalar_tensor_tensor(
                                out=xres[:, fc, tt * W:(tt + 1) * W], in0=scr[:, i, 0:W], scalar=MB_AP,
                                in1=xres[:, fc, tt * W:(tt + 1) * W], op0=ALU.mult, op1=ALU.add),
                                 reads=[("sc", i), "cp", ("x", fc, tt)], writes=[("x", fc, tt)])
                for li, L in enumerate(layers):
                    layer(ch, li, L, li == 0, ((L + 1) * LW + C_GMIX) if li + 1 < nl else None)
                if pipe and ch == 0:
                    for (tl, key) in ((hst, "hs"), (c4h, "c4"), (c3h, "c3")):
                        P.op("dve", lambda e, tl=tl: e.tensor_scalar(out=tl[:], in0=tl[:], scalar1=MK_AP, scalar2=None, op0=ALU.mult),
                             reads=["cp"] + [(key, l, f) for l in range(nl) for f in range(NFC)],
                             writes=[(key, l, f) for l in range(nl) for f in range(NFC)])
                if pipe:
                    for q in range(4):
                        P.dma("sp", "exi%d" % q, lambda e, q=q: e.dma_start(out=exin[q].ap().rearrange("(k p) n -> p k n", p=128),
                                                                            in_=xres[:, 2 * q:2 * q + 2, :]),
                              reads=xall(2 * q) + xall(2 * q + 1), writes=[("exi", q)])
                    for q in range(4):
                        P.cc("pool", "ag%d" % q, lambda e, q=q: e.collective_compute("AllGather", ALU.bypass,
                                                                          replica_groups=[[2 * i, 2 * i + 1] for i in range(npairs)],
                                                                          ins=[exin[q].ap()], outs=[exout[q].ap()]),
                             reads=[("exi", q)], writes=[("exo", q)])
                norm(gcol, out_fn=lambda fc, ts: xres[:, fc, ts], out_res=lambda fc, tt: ("x", fc, tt))
                for fc in range(NFC):
                    out_tokens.append(P.dma("sp", "xout%d" % (fc % 4), lambda e, fc=fc, ch=ch: e.dma_start(out=oT[ch, fc], in_=xres[:, fc, :]),
                                            reads=xall(fc)))
            P.wait_all("sp", out_tokens)

        rec = []
        emit_all(DryProg(), rec, None)
        P = Prog(nc)
        emit_all(P, None, rec)
        P.emit()
    return nc


def _pack_cp(inp, layer_ids, mb, mkeep):
    nl = len(layer_ids)
    ncp = nl * LW + 16
    cp = np.zeros((128, ncp), np.float32)

    def put(col, vec):
        cp[:, col:col + 8] = np.asarray(vec, np.float32).reshape(8, 128).T

    for li, L in enumerate(layer_ids):
        o = li * LW
        put(o + C_GMIX, inp["g_mix"][L])
        put(o + C_GFFN, inp["g_ffn"][L])
        put(o + C_GPLE, inp["g_ple"][L])
        for k in range(4):
            put(o + C_C4W + 8 * k, inp["conv4_w"][L, k])
        put(o + C_C4B, inp["conv4_b"][L])
        put(o + C_BRR, np.asarray(inp["b_rg_r"][L]).reshape(-1))
        put(o + C_BRI, np.asarray(inp["b_rg_i"][L]).reshape(-1))
        put(o + C_LAM, inp["lru_lambda"][L])
        for k in range(3):
            put(o + C_C3W + 8 * k, inp["conv3_w"][L, k])
    put(nl * LW, inp["g_final"])
    cp[:, nl * LW + 8] = mb
    cp[:, nl * LW + 9] = mkeep
    return cp


_CACHE = {}
_WNAMES = ("w_in", "w_rg_r", "w_rg_i", "w_out", "w_gate_up", "w_down", "w_ple_gate", "w_ple")


def _prog(nch, nl, pipe, npairs=4):
    key = (nch, nl, pipe, npairs)
    if key not in _CACHE:
        _CACHE[key] = build_program(nch, nl, pipe, npairs)
    return _CACHE[key]


def _run_solo(inp, nseq, nch):
    nc = _prog(nch, DEPTH, False)
    ids = list(range(DEPTH))
    cp = _pack_cp(inp, ids, 0.0, 1.0)
    f32 = lambda a: np.ascontiguousarray(np.asarray(a, dtype=np.float32))
    wts = {k: f32(inp[k]) for k in _WNAMES}
    x = np.asarray(inp["x"], np.float32)
    p = np.asarray(inp["p"], np.float32)
    in_maps = []
    for b in range(nseq):
        xb = x[b, :nch * TC].reshape(nch, TC, NFC, 128).transpose(0, 2, 3, 1)
        pb = p[:, b, :nch * TC].reshape(DEPTH, nch, TC, 2, 128).transpose(0, 1, 3, 4, 2)
        m = {"xT": np.ascontiguousarray(xb), "pT": np.ascontiguousarray(pb), "cp": cp}
        m.update(wts)
        in_maps.append(m)
    res = run_bass_kernel_spmd(nc, in_maps, core_ids=list(range(nseq)))
    outs = []
    for b in range(nseq):
        o = np.asarray(res.results[b]["oT"], np.float32)
        outs.append(o.transpose(0, 3, 1, 2).reshape(nch * TC, D))
    return np.stack(outs, 0)


def _run_pipe(inp, nseq, nch):
    nit = nch + 1
    nl = DEPTH // 2
    nc = _prog(nit, nl, True, nseq)
    f32 = lambda a: np.ascontiguousarray(np.asarray(a, dtype=np.float32))
    x = np.asarray(inp["x"], np.float32)
    p = np.asarray(inp["p"], np.float32)
    in_maps = []
    for b in range(nseq):
        for half in range(2):
            ids = [half * nl + i for i in range(nl)]
            xT = np.zeros((nit, NFC, 128, TC), np.float32)
            pT = np.zeros((nl, nit, 2, 128, TC), np.float32)
            pb = p[ids][:, b, :nch * TC].reshape(nl, nch, TC, 2, 128).transpose(0, 1, 3, 4, 2)
            if half == 0:
                xT[:nch] = x[b, :nch * TC].reshape(nch, TC, NFC, 128).transpose(0, 2, 3, 1)
                pT[:, :nch] = pb
            else:
                pT[:, 1:] = pb
            m = {"xT": xT, "pT": pT, "cp": _pack_cp(inp, ids, float(half), float(1 - half))}
            for k in _WNAMES:
                m[k] = f32(np.asarray(inp[k])[ids])
            in_maps.append(m)
    res = run_bass_kernel_spmd(nc, in_maps, core_ids=list(range(2 * nseq)))
    _CACHE["last"] = res
    outs = []
    for b in range(nseq):
        o = np.asarray(res.results[2 * b + 1]["oT"], np.float32)[1:]
        outs.append(o.transpose(0, 3, 1, 2).reshape(nch * TC, D))
    return np.stack(outs, 0)


def kernel(**inputs):
    out = _run_pipe(inputs, BATCH, SEQ // TC)
    return np.ascontiguousarray(out.astype(np.float32))
```

```python
import math
from contextlib import ExitStack

import numpy as np
import concourse.bass as bass
import concourse.mybir as mybir
from concourse.bass_utils import run_bass_kernel_spmd

F32 = mybir.dt.float32
BF16 = mybir.dt.bfloat16
AF = mybir.ActivationFunctionType
ALU = mybir.AluOpType

D = 1024
NFC = 8
DFF = 2816
NJ = 22
PLE = 256
DEPTH = 4
SEQ = 8192
BATCH = 4
TC = 1024
W = 512
NT = TC // W
EPS = 1e-6
NSLOT = 8
NSCR = 12
import os
PENG = os.environ.get('PENG', 'dve')
DEFER = int(os.environ.get('DEFER', '15'))
KAHEAD = int(os.environ.get('KAHEAD', '2'))

C_GMIX, C_GFFN, C_GPLE, C_C4W, C_C4B, C_BRR, C_BRI, C_LAM, C_C3W = 0, 8, 16, 24, 56, 64, 72, 80, 88
C_HC, C_HBR, C_HBI, C_C3W2, C_TMP = 112, 120, 128, 136, 160
LW = 168
GELU_K0 = math.sqrt(2.0 / math.pi)
GELU_K1 = GELU_K0 * 0.044715


class Prog:
    ENGS = ("pe", "act", "dve", "pool", "sp")

    def __init__(self, nc):
        self.nc = nc
        self.ops = {e: [] for e in self.ENGS}
        self.cnt = {}
        self.seen = {}
        self.last_w = {}
        self.readers = {}
        self.semkeys = []
        for e in self.ENGS[:4]:
            self._sem(e)

    def _sem(self, key):
        if key not in self.cnt:
            self.cnt[key] = 0
            self.semkeys.append(key)

    def _deps(self, reads, writes):
        deps = []
        for r in reads:
            t = self.last_w.get(r)
            if t is not None:
                deps.append(t)
        for w in writes:
            t = self.last_w.get(w)
            if t is not None:
                deps.append(t)
            deps.extend(self.readers.get(w, ()))
        return deps

    def _commit(self, token, reads, writes):
        for r in reads:
            self.readers.setdefault(r, []).append(token)
        for w in writes:
            self.last_w[w] = token
            self.readers[w] = []

    def _waits(self, eng, deps, token):
        waits = {}
        for (k, v) in deps:
            if (k, v) == token:
                continue
            if k == "pe" and eng == "pe":
                continue
            if self.seen.get((eng, k), 0) >= v:
                continue
            if waits.get(k, 0) < v:
                waits[k] = v
        for k, v in waits.items():
            self.seen[(eng, k)] = v
        return list(waits.items())

    def op(self, eng, fn, reads=(), writes=(), inc=True):
        deps = self._deps(reads, writes)
        if inc:
            self.cnt[eng] += 1
            token = (eng, self.cnt[eng])
        else:
            token = (eng, self.cnt[eng] + 1)
        waits = self._waits(eng, deps, token)
        self.ops[eng].append((waits, fn, (eng, 1) if inc else None))
        self._commit(token, reads, writes)
        return token

    def dma(self, queue, chan, fn, reads=(), writes=()):
        key = ("dma", chan)
        self._sem(key)
        deps = self._deps(reads, writes)
        if self.cnt[key] > 0:
            deps.append((key, self.cnt[key]))
        self.cnt[key] += 16
        token = (key, self.cnt[key])
        waits = self._waits(queue, deps, token)
        self.ops[queue].append((waits, fn, (key, 16)))
        self._commit(token, reads, writes)
        return token

    def cc(self, queue, chan, fn, reads=(), writes=()):
        key = ("cc", chan)
        self._sem(key)
        deps = self._deps(reads, writes)
        if self.cnt[key] > 0:
            deps.append((key, self.cnt[key]))
        self.cnt[key] += 1
        token = (key, self.cnt[key])
        waits = self._waits(queue, deps, token)
        self.ops[queue].append((waits, fn, (key, 1)))
        self._commit(token, reads, writes)
        return token

    def wait_all(self, eng, tokens):
        waits = self._waits(eng, tokens, None)
        self.ops[eng].append((waits, None, None))

    def emit(self):
        nc = self.nc
        with ExitStack() as es:
            sems = {}
            for i, k in enumerate(self.semkeys):
                sems[k] = es.enter_context(nc.semaphore("s%d" % i))
            block = es.enter_context(nc.Block())

            def run(engname):
                def body(eng):
                    for waits, fn, inc in self.ops[engname]:
                        for k, v in waits:
                            eng.wait_ge(sems[k], v)
                        if fn is not None:
                            ins = fn(eng)
                            if inc is not None:
                                ins.then_inc(sems[inc[0]], inc[1])
                return body

            block.tensor(run("pe"))
            block.scalar(run("act"))
            block.vector(run("dve"))
            block.gpsimd(run("pool"))
            block.sync(run("sp"))


class DryProg:
    def op(self, *a, **k):
        return None

    def dma(self, *a, **k):
        return None

    def cc(self, *a, **k):
        return None

    def wait_all(self, *a, **k):
        pass


def build_program(nch, nl, pipe, npairs=4):
    layers = list(range(nl))
    ncp = nl * LW + 16
    nc = bass.Bass("TRN2", target_bir_lowering=False)
    dt = nc.dram_tensor
    xT = dt("xT", [nch, NFC, 128, TC], F32, kind="ExternalInput").ap()
    pT = dt("pT", [nl, nch, 2, 128, TC], F32, kind="ExternalInput").ap()
    cp = dt("cp", [128, ncp], F32, kind="ExternalInput").ap()
    w_in = dt("w_in", [nl, D, 7 * D], F32, kind="ExternalInput").ap()
    w_rg_r = dt("w_rg_r", [nl, 4, 256, 256], F32, kind="ExternalInput").ap()
    w_rg_i = dt("w_rg_i", [nl, 4, 256, 256], F32, kind="ExternalInput").ap()
    w_out = dt("w_out", [nl, D, D], F32, kind="ExternalInput").ap()
    w_gu = dt("w_gate_up", [nl, D, 2 * DFF], F32, kind="ExternalInput").ap()
    w_dn = dt("w_down", [nl, DFF, D], F32, kind="ExternalInput").ap()
    w_pg = dt("w_ple_gate", [nl, D, D], F32, kind="ExternalInput").ap()
    w_pl = dt("w_ple", [nl, PLE, D], F32, kind="ExternalInput").ap()
    oT = dt("oT", [nch, NFC, 128, TC], F32, kind="ExternalOutput").ap()
    if pipe:
        exin = [nc.dram_tensor("exin%d" % q, [256, TC], F32) for q in range(4)]
        exout = [nc.dram_tensor("exout%d" % q, [512, TC], F32) for q in range(4)]

    es = ExitStack()
    with es:
        def sb(name, shape, dtype):
            return es.enter_context(nc.sbuf_tensor(name, shape, dtype))

        xres = sb("xres", [128, NFC, TC], F32)
        hT = sb("hT", [128, NFC, TC], BF16)
        sq = sb("sq", [128, NFC, W], BF16)
        actT = sb("actT", [128, NJ, TC], BF16)
        ptl = sb("ptl", [128, 2, 2, TC], BF16)
        ring = sb("ring", [128, NSLOT, 8 * 256], BF16)
        cpt = sb("cpt", [128, ncp], F32)
        cst = sb("cst", [128, 4], F32)
        ones = sb("ones", [128, 128], BF16)
        scr = sb("scr", [128, NSCR, 516], F32)
        llA = sb("llA", [128, 4, W], F32)
        llB = sb("llB", [128, 4, W], F32)
        llC = sb("llC", [128, 4, W], F32)
        llD = sb("llD", [128, 4, W], F32)
        xcb = sb("xcb", [128, 4, W], BF16)
        rstd = sb("rstd", [128, 2, W], F32)
        hst = sb("hst", [128, nl, NFC], F32)
        c4h = sb("c4h", [128, nl, NFC, 3], F32)
        c3h = sb("c3h", [128, nl, NFC, 2], F32)
        psb = [es.enter_context(nc.psum_tensor("ps%d" % i, [128, W], F32)) for i in range(8)]

        def emit_all(P, rec, specs):
            st = {"ps": 0, "scr": 0, "ws": 0, "pbuf": 0, "hcnt": 0, "wn": 0, "wi": 0}

            act32 = actT[:, 0:16, :].bitcast(F32) if pipe else None

            def next_ps():
                b = st["ps"]
                st["ps"] = (b + 1) % 8
                return b

            def next_scr():
                i = st["scr"]
                st["scr"] = (i + 1) % NSCR
                return i

            EPS_AP = cst[:, 0:1]
            ONE_AP = cst[:, 1:2]
            ZERO_AP = cst[:, 2:3]

            P.op("dve", lambda e: e.memset(cst[:, 0:1], EPS), writes=["cst"])
            P.op("dve", lambda e: e.memset(cst[:, 1:2], 1.0), writes=["cst"])
            P.op("dve", lambda e: e.memset(cst[:, 2:4], 0.0), writes=["cst"])
            P.op("dve", lambda e: e.memset(ones[:], 1.0 / D), writes=["ones"])
            P.op("dve", lambda e: e.memset(hst[:], 0.0), writes=[("hs", l, f) for l in range(nl) for f in range(NFC)])
            P.op("dve", lambda e: e.memset(c4h[:], 0.0), writes=[("c4", l, f) for l in range(nl) for f in range(NFC)])
            P.op("dve", lambda e: e.memset(c3h[:], 0.0), writes=[("c3", l, f) for l in range(nl) for f in range(NFC)])
            P.dma("sp", "cp", lambda e: e.dma_start(out=cpt[:], in_=cp), writes=["cp"])
            for li, L in enumerate(layers):
                o = L * LW
                P.op("act", lambda e, o=o: e.activation(out=cpt[:, o + C_TMP:o + C_TMP + 8], in_=cpt[:, o + C_LAM:o + C_LAM + 8],
                                                          func=AF.Exp, bias=ZERO_AP, scale=-1.0), reads=["cp", "cst"], writes=["cp"])
                P.op("act", lambda e, o=o: e.activation(out=cpt[:, o + C_TMP:o + C_TMP + 8], in_=cpt[:, o + C_TMP:o + C_TMP + 8],
                                                          func=AF.Ln, bias=ONE_AP, scale=1.0), reads=["cp", "cst"], writes=["cp"])
                P.op("dve", lambda e, o=o: e.tensor_scalar(out=cpt[:, o + C_HC:o + C_HC + 8], in0=cpt[:, o + C_TMP:o + C_TMP + 8],
                                                             scalar1=-4.0, scalar2=None, op0=ALU.mult), reads=["cp"], writes=["cp"])
                P.op("dve", lambda e, o=o: e.tensor_scalar(out=cpt[:, o + C_HBR:o + C_HBR + 16], in0=cpt[:, o + C_BRR:o + C_BRR + 16],
                                                             scalar1=0.5, scalar2=None, op0=ALU.mult), reads=["cp"], writes=["cp"])
                P.op("dve", lambda e, o=o: e.tensor_scalar(out=cpt[:, o + C_C3W2:o + C_C3W2 + 24], in0=cpt[:, o + C_C3W:o + C_C3W + 24],
                                                             scalar1=2.0, scalar2=None, op0=ALU.mult), reads=["cp"], writes=["cp"])

            def wslot(dmas):
                if rec is not None:
                    rec.append(dmas)
                    s = st["ws"]
                    st["ws"] = (s + 1) % NSLOT
                    return s
                n = st["wn"]
                st["wn"] += 1
                while st["wi"] <= min(n + KAHEAD, len(specs) - 1):
                    m = st["wi"]
                    st["wi"] += 1
                    sl = m % NSLOT
                    for dst_fn, src in specs[m]:
                        P.dma("pool", "w%d" % sl, lambda e, d=dst_fn(ring[:, sl, :]), src=src: e.dma_start(out=d, in_=src),
                              writes=[("ws", sl)])
                return n % NSLOT

            def kview(ap, nk, ncol):
                return ap[:, 0:nk * ncol].rearrange("p (k n) -> p k n", k=nk)

            def wsrc(mat, r0, nk, c0, ncol):
                return mat[r0:r0 + nk * 128, c0:c0 + ncol].rearrange("(k p) n -> p k n", p=128)

            def mm_group(s, kcs, col0, rhs_fn, rhs_res, bank, first=True, last=True, koff=0):
                n = len(kcs)
                for i, k in enumerate(kcs):
                    P.op("pe", lambda e, k=k, i=i: e.matmul(psb[bank][:], kview(ring[:, s, :], 8, 256)[:, k - koff, col0:col0 + 128],
                                                             rhs_fn(k), start=(first and i == 0), stop=(last and i == n - 1)),
                         reads=[("ws", s)] + rhs_res(k), writes=[("ps", bank)], inc=(i == n - 1))

            def norm_sqs(tt):
                ts = slice(tt * W, (tt + 1) * W)
                for fc in range(NFC):
                    if fc % 2 == 0:
                        P.op("act", lambda e, fc=fc: e.activation(out=sq[:, fc, :], in_=xres[:, fc, ts], func=AF.Square,
                                                                  bias=ZERO_AP, scale=1.0),
                             reads=[("x", fc, tt), "cst"], writes=[("sq", fc)])
                    else:
                        P.op("dve", lambda e, fc=fc: e.tensor_tensor(out=sq[:, fc, :], in0=xres[:, fc, ts], in1=xres[:, fc, ts], op=ALU.mult),
                             reads=[("x", fc, tt)], writes=[("sq", fc)])

            def norm_mm(tt):
                b = next_ps()
                for fc in range(NFC):
                    P.op("pe", lambda e, fc=fc: e.matmul(psb[b][:], ones[:], sq[:, fc, :], start=(fc == 0), stop=(fc == NFC - 1)),
                         reads=["ones", ("sq", fc)], writes=[("ps", b)], inc=(fc == NFC - 1))
                return b

            def norm_fin(gcol, tt, b, out_fn=None, out_res=None):
                ts = slice(tt * W, (tt + 1) * W)
                P.op("act", lambda e: e.activation(out=rstd[:, tt, :], in_=psb[b][:], func=AF.Sqrt, bias=EPS_AP, scale=1.0),
                     reads=[("ps", b), "cst"], writes=[("rstd", tt)])
                P.op("dve", lambda e: e.reciprocal(out=rstd[:, tt, :], in_=rstd[:, tt, :]), reads=[("rstd", tt)], writes=[("rstd", tt)])
                for fc in range(NFC):
                    dst = hT[:, fc, ts] if out_fn is None else out_fn(fc, ts)
                    dres = ("h", fc, tt) if out_fn is None else out_res(fc, tt)
                    P.op("dve", lambda e, fc=fc, dst=dst: e.scalar_tensor_tensor(out=dst, in0=xres[:, fc, ts],
                                                                                 scalar=cpt[:, gcol + fc:gcol + fc + 1], in1=rstd[:, tt, :],
                                                                                 op0=ALU.mult, op1=ALU.mult),
                         reads=[("x", fc, tt), ("rstd", tt), "cp"], writes=[dres])

            def norm(gcol, out_fn=None, out_res=None):
                for tt in range(NT):
                    norm_sqs(tt)
                    b = norm_mm(tt)
                    norm_fin(gcol, tt, b, out_fn, out_res)

            def h_rhs(tt):
                return (lambda k: hT[:, k, tt * W:(tt + 1) * W]), (lambda k: [("h", k, tt)])

            def layer(ch, li, L, first, nxt, prefetch_next, xchg):
                o = L * LW
                col = lambda c, fc: cpt[:, o + c + fc:o + c + fc + 1]
                if first:
                    norm(o + C_GMIX)
                def do_head(hh, TA, nA, TD, nD):
                    units = [(cc, tt) for tt in range(NT) for cc in range(2)]
                    U = lambda cc, tt: cc * NT + tt
                    TB, nB, TC, nC = llB, "llB", llC, "llC"

                    def zslot(g):
                        return wslot([(lambda d: kview(d, 8, 256), wsrc(w_in[L], 0, 8, g * D + hh * 256, 256))])

                    def zrun(s, evac, tts):
                        pend = None
                        for tt in tts:
                            for cc in range(2):
                                b = next_ps()
                                rf, rr = h_rhs(tt)
                                mm_group(s, list(range(8)), cc * 128, rf, rr, b)
                                d = evac(cc, tt, b)
                                if pend is not None:
                                    pend()
                                pend = d
                        if pend is not None:
                            pend()

                    def zgroup(g, evac):
                        zrun(zslot(g), evac, range(NT))

                    def ev_rnnx(cc, tt, b):
                        fc = hh * 2 + cc
                        u = U(cc, tt)
                        i = next_scr()
                        P.op("act", lambda e: e.activation(out=scr[:, i, 0:3], in_=c4h[:, li, fc, :], func=AF.Copy),
                             reads=[("c4", li, fc)], writes=[("sch", i), ("sc", i)])
                        P.op("act", lambda e: e.activation(out=scr[:, i, 3:3 + W], in_=psb[b][:], func=AF.Copy),
                             reads=[("ps", b)], writes=[("sc", i)])
                        P.op("act", lambda e: e.activation(out=TA[:, u, :], in_=psb[b][:], func=AF.Identity,
                                                           bias=col(C_C4B, fc), scale=col(C_C4W + 24, fc)),
                             reads=[("ps", b), "cp"], writes=[(nA, u)])
                        P.op("act", lambda e: e.activation(out=c4h[:, li, fc, :], in_=scr[:, i, W:W + 3], func=AF.Copy),
                             reads=[("sc", i)], writes=[("c4", li, fc)])
                        def later():
                            for k in (2, 1, 0):
                                P.op("dve", lambda e, k=k: e.scalar_tensor_tensor(out=TA[:, u, :], in0=scr[:, i, k:k + W],
                                                                                  scalar=col(C_C4W + 8 * k, fc), in1=TA[:, u, :],
                                                                                  op0=ALU.mult, op1=ALU.add),
                                     reads=[("sc", i), ("sch", i), (nA, u), "cp"], writes=[(nA, u)])
                            P.op("act", lambda e: e.activation(out=xcb[:, u, :], in_=TA[:, u, :], func=AF.Copy),
                                 reads=[(nA, u)], writes=[("xcb", u)])
                        if DEFER & 1:
                            return later
                        later()
                        return None

                    def ev_grnn(cc, tt, b):
                        u = U(cc, tt)
                        P.op("act", lambda e: e.activation(out=TD[:, u, :], in_=psb[b][:], func=AF.Tanh, bias=ZERO_AP, scale=0.5),
                             reads=[("ps", b), "cst"], writes=[(nD, u)])

                    def ev_rnny(cc, tt, b):
                        u = U(cc, tt)
                        P.op("act", lambda e: e.activation(out=TC[:, u, :], in_=psb[b][:], func=AF.Gelu_apprx_tanh, bias=ZERO_AP, scale=1.0),
                             reads=[("ps", b), "cst"], writes=[(nC, u)])
                        later = lambda: P.op("dve", lambda e: e.scalar_tensor_tensor(out=TC[:, u, :], in0=TD[:, u, :], scalar=1.0, in1=TC[:, u, :],
                                                                                     op0=ALU.add, op1=ALU.mult),
                                             reads=[(nD, u), (nC, u)], writes=[(nC, u)])
                        if DEFER & 2:
                            return later
                        later()
                        return None

                    def ev_convb(cc, tt, b):
                        u = U(cc, tt)
                        P.op("act", lambda e: e.activation(out=TB[:, u, :], in_=psb[b][:], func=AF.Copy),
                             reads=[("ps", b)], writes=[(nB, u)])

                    def ev_convc(cc, tt, b):
                        u = U(cc, tt)
                        P.op("act", lambda e: e.activation(out=TD[:, u, :], in_=psb[b][:], func=AF.Copy),
                             reads=[("ps", b)], writes=[(nD, u)])

                    def ev_convx(cc, tt, b):
                        fc = hh * 2 + cc
                        j = next_scr()
                        u = U(cc, tt)
                        P.op("act", lambda e: e.activation(out=scr[:, j, 0:2], in_=c3h[:, li, fc, :], func=AF.Copy),
                             reads=[("c3", li, fc)], writes=[("sch", j), ("sc", j)])
                        P.op("dve", lambda e: e.tensor_tensor(out=scr[:, j, 2:2 + W], in0=TD[:, u, :], in1=psb[b][:], op=ALU.mult),
                             reads=[(nD, u), ("ps", b)], writes=[("sc", j)])
                        def later():
                            P.op("act", lambda e: e.activation(out=c3h[:, li, fc, :], in_=scr[:, j, W:W + 2], func=AF.Copy),
                                 reads=[("sc", j)], writes=[("c3", li, fc)])
                            P.op("act", lambda e: e.activation(out=TD[:, u, :], in_=scr[:, j, 2:2 + W], func=AF.Identity,
                                                               bias=ZERO_AP, scale=col(C_C3W + 16, fc)),
                                 reads=[("sc", j), "cp", "cst"], writes=[(nD, u)])
                            for k in (1, 0):
                                P.op("dve", lambda e, k=k: e.scalar_tensor_tensor(out=TD[:, u, :], in0=scr[:, j, k:k + W],
                                                                                  scalar=col(C_C3W + 8 * k, fc), in1=TD[:, u, :],
                                                                                  op0=ALU.mult, op1=ALU.add),
                                     reads=[("sc", j), ("sch", j), (nD, u), "cp"], writes=[(nD, u)])
                            P.op(PENG, lambda e: e.tensor_tensor(out=TB[:, u, :], in0=TD[:, u, :], in1=TB[:, u, :], op=ALU.mult),
                                 reads=[(nD, u), (nB, u)], writes=[(nB, u)])
                        if DEFER & 4:
                            return later
                        later()
                        return None

                    def ev_gconv(cc, tt, b):
                        fc = hh * 2 + cc
                        u = U(cc, tt)
                        i = next_scr()
                        P.op("act", lambda e: e.activation(out=scr[:, i, 0:W], in_=psb[b][:], func=AF.Tanh, bias=ZERO_AP, scale=0.5),
                             reads=[("ps", b), "cst"], writes=[("sc", i)])
                        def later():
                            P.op("dve", lambda e: e.scalar_tensor_tensor(out=TB[:, u, :], in0=scr[:, i, 0:W], scalar=1.0, in1=TB[:, u, :],
                                                                         op0=ALU.add, op1=ALU.mult),
                                 reads=[("sc", i), (nB, u)], writes=[(nB, u)])
                            P.op(PENG, lambda e: e.tensor_tensor(out=TA[:, u, :], in0=TA[:, u, :], in1=TC[:, u, :], op=ALU.mult),
                                 reads=[(nA, u), (nC, u)], writes=[(nA, u)])
                            P.op(PENG, lambda e: e.tensor_tensor(out=actT[:, fc, tt * W:(tt + 1) * W], in0=TA[:, u, :], in1=TB[:, u, :], op=ALU.add),
                                 reads=[(nA, u), (nB, u)], writes=[("act", fc, tt)])
                        if DEFER & 8:
                            return later
                        later()
                        return None

                    gsc = {}

                    def gates_a(s, cc, tt):
                        fc = hh * 2 + cc
                        xrf = lambda k: xcb[:, (k % 2) * NT + tt, :]
                        xrr = lambda k: [("xcb", (k % 2) * NT + tt)]
                        br = next_ps()
                        mm_group(s, [0, 1], cc * 128, xrf, xrr, br)
                        bi = next_ps()
                        mm_group(s, [2, 3], cc * 128, xrf, xrr, bi)
                        i1, i2, i3 = next_scr(), next_scr(), next_scr()
                        gsc[(cc, tt)] = (i1, i2, i3)
                        P.op("act", lambda e: e.activation(out=scr[:, i1, 0:W], in_=psb[br][:], func=AF.Tanh, bias=col(C_HBR, fc), scale=0.5),
                             reads=[("ps", br), "cp"], writes=[("sc", i1)])
                        P.op("act", lambda e: e.activation(out=scr[:, i2, 0:W], in_=scr[:, i1, 0:W], func=AF.Exp, bias=col(C_HC, fc),
                                                           scale=col(C_HC, fc)),
                             reads=[("sc", i1), "cp"], writes=[("sc", i2)])
                        P.op("act", lambda e: e.activation(out=scr[:, i3, 0:W], in_=psb[bi][:], func=AF.Tanh, bias=col(C_HBI, fc), scale=0.5),
                             reads=[("ps", bi), "cp"], writes=[("sc", i3)])
                        P.op(PENG, lambda e: e.tensor_tensor(out=scr[:, i1, 0:W], in0=scr[:, i2, 0:W], in1=scr[:, i2, 0:W], op=ALU.mult),
                             reads=[("sc", i2), ("sc", i1)], writes=[("sc", i1)])

                    def gates_b(cc, tt):
                        i1, i2, i3 = gsc[(cc, tt)]
                        P.op("act", lambda e: e.activation(out=scr[:, i1, 0:W], in_=scr[:, i1, 0:W], func=AF.Sqrt, bias=ONE_AP, scale=-1.0),
                             reads=[("sc", i1), "cst"], writes=[("sc", i1)])

                    def gates_c(cc, tt):
                        fc = hh * 2 + cc
                        u = U(cc, tt)
                        i1, i2, i3 = gsc[(cc, tt)]
                        P.op("dve", lambda e: e.scalar_tensor_tensor(out=scr[:, i3, 0:W], in0=scr[:, i3, 0:W], scalar=1.0, in1=TA[:, u, :],
                                                                     op0=ALU.add, op1=ALU.mult),
                             reads=[("sc", i3), (nA, u)], writes=[("sc", i3)])
                        P.op("dve", lambda e: e.scalar_tensor_tensor(out=scr[:, i3, 0:W], in0=scr[:, i3, 0:W], scalar=0.5, in1=scr[:, i1, 0:W],
                                                                     op0=ALU.mult, op1=ALU.mult),
                             reads=[("sc", i3), ("sc", i1)], writes=[("sc", i3)])
                        P.op("dve", lambda e: e.tensor_tensor_scan(out=TA[:, u, :], data0=scr[:, i2, 0:W], data1=scr[:, i3, 0:W],
                                                                   initial=hst[:, li, fc:fc + 1], op0=ALU.mult, op1=ALU.add),
                             reads=[("sc", i2), ("sc", i3), ("hs", li, fc), (nA, u)], writes=[(nA, u)])
                        P.op("dve", lambda e: e.tensor_copy(out=hst[:, li, fc:fc + 1], in_=TA[:, u, W - 1:W]),
                             reads=[(nA, u)], writes=[("hs", li, fc)])

                    if hh == 0:
                        s0, s5, s1 = zslot(0), zslot(5), zslot(1)
                        for tts in ((0,), (1,)):
                            zrun(s0, ev_rnnx, tts)
                            zrun(s5, ev_grnn, tts)
                            zrun(s1, ev_rnny, tts)
                    else:
                        zgroup(0, ev_rnnx)
                        zgroup(5, ev_grnn)
                        zgroup(1, ev_rnny)
                    sg = wslot([(lambda d: kview(d, 8, 256)[:, 0:2, :], wsrc(w_rg_r[L, hh], 0, 2, 0, 256)),
                                (lambda d: kview(d, 8, 256)[:, 2:4, :], wsrc(w_rg_i[L, hh], 0, 2, 0, 256))])
                    for cc, tt in units:
                        gates_a(sg, cc, tt)
                    for cc, tt in units:
                        gates_b(cc, tt)
                    for cc, tt in units:
                        gates_c(cc, tt)
                    zgroup(2, ev_convb)
                    zgroup(3, ev_convc)
                    zgroup(4, ev_convx)
                    zgroup(6, ev_gconv)

                for hh in range(4):
                    par = st["hcnt"] % 2
                    st["hcnt"] += 1
                    if par == 0:
                        do_head(hh, llA, "llA", llD, "llD")
                    else:
                        do_head(hh, llD, "llD", llA, "llA")

                oslots = [wslot([(lambda d: kview(d, 8, 256), wsrc(w_out[L], 0, 8, cs * 256, 256))]) for cs in range(4)]

                def out_part(tt, css):
                    for cs in css:
                        for cc in range(2):
                            fc = cs * 2 + cc
                            b = next_ps()
                            mm_group(oslots[cs], list(range(8)), cc * 128, lambda k: actT[:, k, tt * W:(tt + 1) * W],
                                     lambda k: [("act", k, tt)], b)
                            P.op("dve", lambda e, fc=fc, b=b: e.scalar_tensor_tensor(
                                out=xres[:, fc, tt * W:(tt + 1) * W], in0=psb[b][:], scalar=0.5, in1=xres[:, fc, tt * W:(tt + 1) * W],
                                op0=ALU.mult, op1=ALU.add), reads=[("ps", b), ("x", fc, tt)], writes=[("x", fc, tt)])

                pb = st["pbuf"]
                st["pbuf"] = 1 - pb
                for kc in range(2):
                    P.dma("pool", "p%d_%d" % (pb, kc), lambda e, kc=kc: e.dma_start(out=ptl[:, pb, kc, :], in_=pT[L, ch, kc]),
                          writes=[("p", pb, kc)])
                out_part(0, range(4))
                norm_sqs(0)
                out_part(1, (0, 1))
                nb0 = norm_mm(0)
                norm_fin(o + C_GFFN, 0, nb0)
                out_part(1, (2, 3))
                norm_sqs(1)
                nb1 = norm_mm(1)
                norm_fin(o + C_GFFN, 1, nb1)

                def up_slot(j):
                    return wslot([(lambda d: kview(d, 8, 256)[:, :, 0:128], wsrc(w_gu[L], 0, 8, j * 128, 128)),
                                  (lambda d: kview(d, 8, 256)[:, :, 128:256], wsrc(w_gu[L], 0, 8, DFF + j * 128, 128))])

                def up_unit(s, j, tt):
                    rf, rr = h_rhs(tt)
                    bg = next_ps()
                    mm_group(s, list(range(8)), 0, rf, rr, bg)
                    bu = next_ps()
                    mm_group(s, list(range(8)), 128, rf, rr, bu)
                    i = next_scr()
                    P.op("act", lambda e: e.activation(out=scr[:, i, 0:W], in_=psb[bg][:], func=AF.Tanh, bias=ZERO_AP, scale=0.5),
                         reads=[("ps", bg), "cst"], writes=[("sc", i)])
                    P.op("dve", lambda e: e.scalar_tensor_tensor(out=scr[:, i, 0:W], in0=scr[:, i, 0:W], scalar=1.0, in1=psb[bg][:],
                                                                 op0=ALU.add, op1=ALU.mult),
                         reads=[("sc", i), ("ps", bg)], writes=[("sc", i)])
                    P.op("dve", lambda e: e.tensor_tensor(out=actT[:, j, tt * W:(tt + 1) * W], in0=scr[:, i, 0:W], in1=psb[bu][:], op=ALU.mult),
                         reads=[("sc", i), ("ps", bu)], writes=[("act", j, tt)])

                JB = 4
                first_slots = [up_slot(j) for j in range(JB)]
                for tt in range(NT):
                    for j in range(JB):
                        up_unit(first_slots[j], j, tt)
                for j in range(JB, NJ):
                    s = up_slot(j)
                    for tt in range(NT):
                        up_unit(s, j, tt)
                for cs in range(4):
                    banks = {}
                    for kg in range(3):
                        k0 = kg * 8
                        nk = min(8, NJ - k0)
                        s = wslot([(lambda d, nk=nk: kview(d, 8, 256)[:, 0:nk, :], wsrc(w_dn[L], k0 * 128, nk, cs * 256, 256))])
                        for cc in range(2):
                            for tt in range(NT):
                                if kg == 0:
                                    banks[(cc, tt)] = next_ps()
                                b = banks[(cc, tt)]
                                mm_group(s, list(range(k0, k0 + nk)), cc * 128, lambda k, tt=tt: actT[:, k, tt * W:(tt + 1) * W],
                                         lambda k, tt=tt: [("act", k, tt)], b, first=(kg == 0), last=(kg == 2), koff=k0)
                    for cc in range(2):
                        fc = cs * 2 + cc
                        for tt in range(NT):
                            b = banks[(cc, tt)]
                            P.op("dve", lambda e, fc=fc, tt=tt, b=b: e.scalar_tensor_tensor(
                                out=xres[:, fc, tt * W:(tt + 1) * W], in0=psb[b][:], scalar=0.5, in1=xres[:, fc, tt * W:(tt + 1) * W],
                                op0=ALU.mult, op1=ALU.add), reads=[("ps", b), ("x", fc, tt)], writes=[("x", fc, tt)])

                if prefetch_next:
                    for fc in range(NFC):
                        P.dma("sp", "xin%d" % (fc % 4), lambda e, fc=fc: e.dma_start(
                            out=act32[:, 2 * fc:2 * fc + 2, :], in_=xT[ch + 1, fc].rearrange("p (t w) -> p t w", t=2)),
                              writes=[("act", 2 * fc, 0), ("act", 2 * fc, 1), ("act", 2 * fc + 1, 0), ("act", 2 * fc + 1, 1)])
                norm(o + C_GPLE)
                spl = wslot([(lambda d: kview(d, 2, 1024), wsrc(w_pl[L], 0, 2, 0, 1024))])
                gslots = [wslot([(lambda d: kview(d, 8, 256), wsrc(w_pg[L], 0, 8, cs * 256, 256))]) for cs in range(4)]

                def ple_part(tt, css):
                    for cs in css:
                        for cc in range(2):
                            ple_unit(tt, cs, cc)

                def ple_unit(tt, cs, cc):
                    fc = cs * 2 + cc
                    rf, rr = h_rhs(tt)
                    bg = next_ps()
                    mm_group(gslots[cs], list(range(8)), cc * 128, rf, rr, bg)
                    bp = next_ps()
                    for kc in range(2):
                        P.op("pe", lambda e, kc=kc: e.matmul(
                            psb[bp][:], kview(ring[:, spl, :], 2, 1024)[:, kc, fc * 128:(fc + 1) * 128],
                            ptl[:, pb, kc, tt * W:(tt + 1) * W], start=(kc == 0), stop=(kc == 1)),
                            reads=[("ws", spl), ("p", pb, kc)], writes=[("ps", bp)], inc=(kc == 1))
                    i = next_scr()
                    P.op("act", lambda e: e.activation(out=scr[:, i, 0:W], in_=psb[bg][:], func=AF.Tanh, bias=ZERO_AP, scale=0.5),
                         reads=[("ps", bg), "cst"], writes=[("sc", i)])
                    P.op("dve", lambda e: e.scalar_tensor_tensor(out=scr[:, i, 0:W], in0=scr[:, i, 0:W], scalar=1.0, in1=psb[bp][:],
                                                                 op0=ALU.add, op1=ALU.mult),
                         reads=[("sc", i), ("ps", bp)], writes=[("sc", i)])
                    P.op("dve", lambda e: e.scalar_tensor_tensor(
                        out=xres[:, fc, tt * W:(tt + 1) * W], in0=scr[:, i, 0:W], scalar=0.5, in1=xres[:, fc, tt * W:(tt + 1) * W],
                        op0=ALU.mult, op1=ALU.add), reads=[("sc", i), ("x", fc, tt)], writes=[("x", fc, tt)])

                if nxt is None and xchg is not None:
                    for cs in range(4):
                        ple_part(0, (cs,))
                        ple_part(1, (cs,))
                        xchg(cs)
                elif nxt is None:
                    ple_part(0, range(4))
                    ple_part(1, range(4))
                else:
                    ple_part(0, range(4))
                    norm_sqs(0)
                    ple_part(1, (0, 1))
                    mb0 = norm_mm(0)
                    norm_fin(nxt, 0, mb0)
                    ple_part(1, (2, 3))
                    norm_sqs(1)
                    mb1 = norm_mm(1)
                    norm_fin(nxt, 1, mb1)

            out_tokens = []
            gcol = nl * LW
            MB_AP = cpt[:, gcol + 8:gcol + 9]
            MK_AP = cpt[:, gcol + 9:gcol + 10]
            xall = lambda fc: [("x", fc, 0), ("x", fc, 1)]
            def final_tile(ch, tt):
                ts = slice(tt * W, (tt + 1) * W)
                norm_sqs(tt)
                b = norm_mm(tt)
                P.op("act", lambda e: e.activation(out=rstd[:, tt, :], in_=psb[b][:], func=AF.Sqrt, bias=EPS_AP, scale=1.0),
                     reads=[("ps", b), "cst"], writes=[("rstd", tt)])
                P.op("dve", lambda e: e.reciprocal(out=rstd[:, tt, :], in_=rstd[:, tt, :]), reads=[("rstd", tt)], writes=[("rstd", tt)])
                for fc in range(NFC):
                    i = next_scr()
                    P.op("dve", lambda e, fc=fc, i=i: e.scalar_tensor_tensor(out=scr[:, i, 0:W], in0=xres[:, fc, ts],
                                                                             scalar=cpt[:, gcol + fc:gcol + fc + 1], in1=rstd[:, tt, :],
                                                                             op0=ALU.mult, op1=ALU.mult),
                         reads=[("x", fc, tt), ("rstd", tt), "cp"], writes=[("sc", i)])
                    out_tokens.append(P.dma("sp", "xout%d" % (i % 4), lambda e, fc=fc, i=i: e.dma_start(out=oT[ch, fc][:, ts], in_=scr[:, i, 0:W]),
                                            reads=[("sc", i)]))

            def xchg(q):
                P.dma("sp", "exi%d" % q, lambda e: e.dma_start(out=exin[q].ap().rearrange("(k p) n -> p k n", p=128),
                                                               in_=xres[:, 2 * q:2 * q + 2, :]),
                      reads=xall(2 * q) + xall(2 * q + 1), writes=[("exi", q)])
                P.cc("pool", "ag%d" % q, lambda e: e.collective_compute("AllGather", ALU.bypass,
                                                                        replica_groups=[[2 * i, 2 * i + 1] for i in range(npairs)],
                                                                        ins=[exin[q].ap()], outs=[exout[q].ap()]),
                     reads=[("exi", q)], writes=[("exo", q)])

            for ch in range(nch):
                if not (pipe and ch >= 1):
                    for fc in range(NFC):
                        P.dma("sp", "xin%d" % (fc % 4), lambda e, fc=fc, ch=ch: e.dma_start(out=xres[:, fc, :], in_=xT[ch, fc]),
                              writes=xall(fc))
                if pipe and ch >= 1:
                    for fc in range(NFC):
                        q, r = fc // 2, fc % 2
                        for tt in range(NT):
                            i = next_scr()
                            P.dma("sp", "stg%d" % i, lambda e, q=q, r=r, tt=tt, i=i: e.dma_start(
                                out=scr[:, i, 0:W], in_=exout[q].ap()[r * 128:(r + 1) * 128, tt * W:(tt + 1) * W]),
                                  reads=[("exo", q)], writes=[("sc", i)])
                            P.op("dve", lambda e, fc=fc, tt=tt, i=i: e.scalar_tensor_tensor(
                                out=xres[:, fc, tt * W:(tt + 1) * W], in0=scr[:, i, 0:W], scalar=MB_AP,
                                in1=act32[:, 2 * fc + tt, :], op0=ALU.mult, op1=ALU.add),
                                 reads=[("sc", i), "cp", ("act", 2 * fc + tt, 0), ("act", 2 * fc + tt, 1)], writes=[("x", fc, tt)])
                for li, L in enumerate(layers):
                    layer(ch, li, L, li == 0, ((L + 1) * LW + C_GMIX) if li + 1 < nl else None,
                          pipe and li == nl - 1 and ch + 1 < nch, xchg if (pipe and li == nl - 1) else None)
                if pipe and ch == 0:
                    for (tl, key) in ((hst, "hs"), (c4h, "c4"), (c3h, "c3")):
                        P.op("dve", lambda e, tl=tl: e.tensor_scalar(out=tl[:], in0=tl[:], scalar1=MK_AP, scalar2=None, op0=ALU.mult),
                             reads=["cp"] + [(key, l, f) for l in range(nl) for f in range(NFC)],
                             writes=[(key, l, f) for l in range(nl) for f in range(NFC)])
                if not pipe:
                    norm(gcol, out_fn=lambda fc, ts: xres[:, fc, ts], out_res=lambda fc, tt: ("x", fc, tt))
                    for fc in range(NFC):
                        out_tokens.append(P.dma("sp", "xout%d" % (fc % 4), lambda e, fc=fc, ch=ch: e.dma_start(out=oT[ch, fc], in_=xres[:, fc, :]),
                                                reads=xall(fc)))
                else:
                    for tt in range(NT):
                        final_tile(ch, tt)
            P.wait_all("sp", out_tokens)

        rec = []
        emit_all(DryProg(), rec, None)
        P = Prog(nc)
        emit_all(P, None, rec)
        P.emit()
    return nc


def _pack_cp(inp, layer_ids, mb, mkeep):
    nl = len(layer_ids)
    ncp = nl * LW + 16
    cp = np.zeros((128, ncp), np.float32)

    def put(col, vec):
        cp[:, col:col + 8] = np.asarray(vec, np.float32).reshape(8, 128).T

    for li, L in enumerate(layer_ids):
        o = li * LW
        put(o + C_GMIX, inp["g_mix"][L])
        put(o + C_GFFN, inp["g_ffn"][L])
        put(o + C_GPLE, inp["g_ple"][L])
        for k in range(4):
            put(o + C_C4W + 8 * k, inp["conv4_w"][L, k])
        put(o + C_C4B, inp["conv4_b"][L])
        put(o + C_BRR, np.asarray(inp["b_rg_r"][L]).reshape(-1))
        put(o + C_BRI, np.asarray(inp["b_rg_i"][L]).reshape(-1))
        put(o + C_LAM, inp["lru_lambda"][L])
        for k in range(3):
            put(o + C_C3W + 8 * k, inp["conv3_w"][L, k])
    put(nl * LW, inp["g_final"])
    cp[:, nl * LW + 8] = mb
    cp[:, nl * LW + 9] = mkeep
    return cp


_CACHE = {}
_WNAMES = ("w_in", "w_rg_r", "w_rg_i", "w_out", "w_gate_up", "w_down", "w_ple_gate", "w_ple")


def _prog(nch, nl, pipe, npairs=4):
    key = (nch, nl, pipe, npairs)
    if key not in _CACHE:
        _CACHE[key] = build_program(nch, nl, pipe, npairs)
    return _CACHE[key]


def _run_solo(inp, nseq, nch):
    nc = _prog(nch, DEPTH, False)
    ids = list(range(DEPTH))
    cp = _pack_cp(inp, ids, 0.0, 1.0)
    f32 = lambda a: np.ascontiguousarray(np.asarray(a, dtype=np.float32))
    wts = {k: f32(inp[k]) for k in _WNAMES}
    x = np.asarray(inp["x"], np.float32)
    p = np.asarray(inp["p"], np.float32)
    in_maps = []
    for b in range(nseq):
        xb = x[b, :nch * TC].reshape(nch, TC, NFC, 128).transpose(0, 2, 3, 1)
        pb = p[:, b, :nch * TC].reshape(DEPTH, nch, TC, 2, 128).transpose(0, 1, 3, 4, 2)
        m = {"xT": np.ascontiguousarray(xb), "pT": np.ascontiguousarray(pb), "cp": cp}
        m.update(wts)
        in_maps.append(m)
    res = run_bass_kernel_spmd(nc, in_maps, core_ids=list(range(nseq)))
    outs = []
    for b in range(nseq):
        o = np.asarray(res.results[b]["oT"], np.float32)
        outs.append(o.transpose(0, 3, 1, 2).reshape(nch * TC, D))
    return np.stack(outs, 0)


def _run_pipe(inp, nseq, nch):
    nit = nch + 1
    nl = DEPTH // 2
    nc = _prog(nit, nl, True, nseq)
    f32 = lambda a: np.ascontiguousarray(np.asarray(a, dtype=np.float32))
    x = np.asarray(inp["x"], np.float32)
    p = np.asarray(inp["p"], np.float32)
    in_maps = []
    for b in range(nseq):
        for half in range(2):
            ids = [half * nl + i for i in range(nl)]
            xT = np.zeros((nit, NFC, 128, TC), np.float32)
            pT = np.zeros((nl, nit, 2, 128, TC), np.float32)
            pb = p[ids][:, b, :nch * TC].reshape(nl, nch, TC, 2, 128).transpose(0, 1, 3, 4, 2)
            if half == 0:
                xT[:nch] = x[b, :nch * TC].reshape(nch, TC, NFC, 128).transpose(0, 2, 3, 1)
                pT[:, :nch] = pb
            else:
                pT[:, 1:] = pb
            m = {"xT": xT, "pT": pT, "cp": _pack_cp(inp, ids, float(half), float(1 - half))}
            for k in _WNAMES:
                m[k] = f32(np.asarray(inp[k])[ids])
            in_maps.append(m)
    res = run_bass_kernel_spmd(nc, in_maps, core_ids=list(range(2 * nseq)))
    _CACHE["last"] = res
    outs = []
    for b in range(nseq):
        o = np.asarray(res.results[2 * b + 1]["oT"], np.float32)[1:]
        outs.append(o.transpose(0, 3, 1, 2).reshape(nch * TC, D))
    return np.stack(outs, 0)


def kernel(**inputs):
    out = _run_pipe(inputs, BATCH, SEQ // TC)
    return np.ascontiguousarray(out.astype(np.float32))
```
